# Optimizing a Trainium2 kernel written in Bass

```python
import jax, jax.numpy as jnp
from jax import lax
import numpy as np

D_MODEL = 1024
BATCH = 2
SEQ = 8192
DEPTH = 1

N_META = 16
D_MIX = D_MODEL
LRU_WIDTH = D_MIX // 2
LRU_HEADS = 8
LRU_HEAD_DIM = LRU_WIDTH // LRU_HEADS
RG_LRU_C = 8.0
CONV_WIDTH = 4
CONV_LEFT = 2
FOURIER_WIDTH = D_MIX - LRU_WIDTH
FOURIER_GROUPS = 8
FOURIER_GROUP_DIM = FOURIER_WIDTH // FOURIER_GROUPS
D_IN_PROJ = 2 * LRU_WIDTH + FOURIER_WIDTH
D_FF = 2816
EPS = 1e-6

kernel_name = "hybrid_rglru_fnet_macaron_encoder"


def rms_norm(x, g):
    xf = x.astype(jnp.float32)
    y = xf * lax.rsqrt(jnp.mean(xf * xf, axis=-1, keepdims=True) + EPS)
    return (y * g.astype(jnp.float32)).astype(x.dtype)


def swiglu(h, w_in, w_out):
    gate, up = jnp.split(h @ w_in, 2, axis=-1)
    return (jax.nn.silu(gate) * up) @ w_out


def centred_depthwise_conv(x, w, b):
    T = x.shape[1]
    xp = jnp.pad(x, ((0, 0), (CONV_LEFT, CONV_WIDTH - 1 - CONV_LEFT), (0, 0)))
    out = xp[:, 0:T] * w[0]
    for k in range(1, CONV_WIDTH):
        out = out + xp[:, k:k + T] * w[k]
    return out + b


def linear_recurrence(a, b):
    def combine(l, r):
        return (l[0] * r[0], r[0] * l[1] + r[1])
    _, h = lax.associative_scan(combine, (a, b), axis=1)
    return h


def block_diag(x, w, b):
    B, T, _ = x.shape
    H, Dh, _ = w.shape
    y = jnp.einsum('bthi,hij->bthj', x.reshape(B, T, H, Dh), w)
    return y.reshape(B, T, H * Dh) + b


def rg_lru(xc, wa, ba, wx, bx, lam):
    r = jax.nn.sigmoid(block_diag(xc, wa, ba))
    i = jax.nn.sigmoid(block_diag(xc, wx, bx))
    log_a = -RG_LRU_C * r * jax.nn.softplus(-lam)
    a = jnp.exp(log_a)
    mult = jnp.sqrt(-jnp.expm1(2.0 * log_a))
    return linear_recurrence(a, mult * (i * xc))


def fourier_mix(v, w, b):
    B, T, _ = v.shape
    vg = v.astype(jnp.float32).reshape(B, T, FOURIER_GROUPS, FOURIER_GROUP_DIM)
    f = jnp.fft.fft2(vg, axes=(1, 3), norm='ortho').real
    y = jnp.einsum('btgi,gij->btgj', f, w.astype(jnp.float32))
    return y.reshape(B, T, FOURIER_WIDTH) + b.astype(jnp.float32)


def setup_inputs(seed: int = 0) -> dict:
    key = jax.random.key(seed)
    ks = jax.random.split(key, 32)
    nrm = lambda k, shape, scale: jax.random.normal(k, shape, jnp.float32) * scale
    gain = lambda k, shape: 1.0 + 0.02 * jax.random.normal(k, shape, jnp.float32)

    def lru_lambda(k):
        ac = jax.random.uniform(k, (DEPTH, LRU_WIDTH), jnp.float32, 0.9, 0.999)
        a = ac ** (1.0 / RG_LRU_C)
        return jnp.log(a) - jnp.log1p(-a)

    hd = LRU_HEAD_DIM ** -0.5
    return {
        "x": nrm(ks[0], (BATCH, SEQ, D_MODEL), 1.0),
        "meta_tokens": nrm(ks[1], (N_META, D_MODEL), 1.0),
        "norm_ffn1": gain(ks[2], (DEPTH, D_MODEL)),
        "w_ffn1_in": nrm(ks[3], (DEPTH, D_MODEL, 2 * D_FF), D_MODEL ** -0.5),
        "w_ffn1_out": nrm(ks[4], (DEPTH, D_FF, D_MODEL), D_FF ** -0.5),
        "norm_mix": gain(ks[5], (DEPTH, D_MODEL)),
        "w_in": nrm(ks[6], (DEPTH, D_MODEL, D_IN_PROJ), D_MODEL ** -0.5),
        "conv_w": nrm(ks[7], (DEPTH, CONV_WIDTH, LRU_WIDTH), CONV_WIDTH ** -0.5),
        "conv_b": nrm(ks[8], (DEPTH, LRU_WIDTH), 0.02),
        "lru_wa_fwd": nrm(ks[9], (DEPTH, LRU_HEADS, LRU_HEAD_DIM, LRU_HEAD_DIM), hd),
        "lru_ba_fwd": nrm(ks[10], (DEPTH, LRU_WIDTH), 0.02),
        "lru_wx_fwd": nrm(ks[11], (DEPTH, LRU_HEADS, LRU_HEAD_DIM, LRU_HEAD_DIM), hd),
        "lru_bx_fwd": nrm(ks[12], (DEPTH, LRU_WIDTH), 0.02),
        "lru_lambda_fwd": lru_lambda(ks[13]),
        "lru_wa_bwd": nrm(ks[14], (DEPTH, LRU_HEADS, LRU_HEAD_DIM, LRU_HEAD_DIM), hd),
        "lru_ba_bwd": nrm(ks[15], (DEPTH, LRU_WIDTH), 0.02),
        "lru_wx_bwd": nrm(ks[16], (DEPTH, LRU_HEADS, LRU_HEAD_DIM, LRU_HEAD_DIM), hd),
        "lru_bx_bwd": nrm(ks[17], (DEPTH, LRU_WIDTH), 0.02),
        "lru_lambda_bwd": lru_lambda(ks[18]),
        "fourier_w": nrm(ks[19], (DEPTH, FOURIER_GROUPS, FOURIER_GROUP_DIM, FOURIER_GROUP_DIM), FOURIER_GROUP_DIM ** -0.5),
        "fourier_b": nrm(ks[20], (DEPTH, FOURIER_WIDTH), 0.02),
        "norm_lru_out": gain(ks[21], (DEPTH, LRU_WIDTH)),
        "norm_fourier_out": gain(ks[22], (DEPTH, FOURIER_WIDTH)),
        "w_out": nrm(ks[23], (DEPTH, D_MIX, D_MODEL), D_MIX ** -0.5),
        "norm_ffn2": gain(ks[24], (DEPTH, D_MODEL)),
        "w_ffn2_in": nrm(ks[25], (DEPTH, D_MODEL, 2 * D_FF), D_MODEL ** -0.5),
        "w_ffn2_out": nrm(ks[26], (DEPTH, D_FF, D_MODEL), D_FF ** -0.5),
        "norm_final": gain(ks[27], (D_MODEL,)),
    }


def reference(x, meta_tokens, norm_ffn1, w_ffn1_in, w_ffn1_out, norm_mix, w_in, conv_w, conv_b,
              lru_wa_fwd, lru_ba_fwd, lru_wx_fwd, lru_bx_fwd, lru_lambda_fwd,
              lru_wa_bwd, lru_ba_bwd, lru_wx_bwd, lru_bx_bwd, lru_lambda_bwd,
              fourier_w, fourier_b, norm_lru_out, norm_fourier_out, w_out,
              norm_ffn2, w_ffn2_in, w_ffn2_out, norm_final):
    B = x.shape[0]
    meta = jnp.broadcast_to(meta_tokens.astype(x.dtype)[None], (B, N_META, D_MODEL))
    h = jnp.concatenate([meta, x], axis=1)

    for l in range(DEPTH):
        h = h + 0.5 * swiglu(rms_norm(h, norm_ffn1[l]), w_ffn1_in[l], w_ffn1_out[l])

        u = rms_norm(h, norm_mix[l]) @ w_in[l]
        lru_x = u[..., :LRU_WIDTH]
        lru_gate = u[..., LRU_WIDTH:2 * LRU_WIDTH]
        four_v = u[..., 2 * LRU_WIDTH:]

        xc = centred_depthwise_conv(lru_x, conv_w[l], conv_b[l]).astype(jnp.float32)
        h_fwd = rg_lru(xc, lru_wa_fwd[l].astype(jnp.float32), lru_ba_fwd[l].astype(jnp.float32),
                       lru_wx_fwd[l].astype(jnp.float32), lru_bx_fwd[l].astype(jnp.float32),
                       lru_lambda_fwd[l].astype(jnp.float32))
        h_bwd = jnp.flip(rg_lru(jnp.flip(xc, axis=1), lru_wa_bwd[l].astype(jnp.float32),
                                lru_ba_bwd[l].astype(jnp.float32), lru_wx_bwd[l].astype(jnp.float32),
                                lru_bx_bwd[l].astype(jnp.float32), lru_lambda_bwd[l].astype(jnp.float32)), axis=1)
        y_lru = (h_fwd + h_bwd) * jax.nn.gelu(lru_gate.astype(jnp.float32))
        y_lru = rms_norm(y_lru, norm_lru_out[l]).astype(h.dtype)

        y_four = rms_norm(fourier_mix(four_v, fourier_w[l], fourier_b[l]), norm_fourier_out[l]).astype(h.dtype)

        h = h + jnp.concatenate([y_lru, y_four], axis=-1) @ w_out[l]

        h = h + 0.5 * swiglu(rms_norm(h, norm_ffn2[l]), w_ffn2_in[l], w_ffn2_out[l])

    return rms_norm(h, norm_final)[:, N_META:]
```

```python
import math
from contextlib import ExitStack

import numpy as np
import concourse.bass as bass
import concourse.mybir as mybir
from concourse.bass_utils import run_bass_kernel_spmd

F32 = mybir.dt.float32
BF16 = mybir.dt.bfloat16
U32 = mybir.dt.uint32
ALU = mybir.AluOpType
AF = mybir.ActivationFunctionType

D = 1024
DFF = 2816
NF = 22
SEQ = 8192
NM = 16
T = SEQ + NM
TOK = 2048
NCOL = TOK + NM
N1, N2 = 108, 76
NV = 56
GROUPS = [[0, 1, 2, 3], [4, 5, 6, 7]]
EPS = 1e-6
DEBUG = False

C_G1, C_GM, C_G2, C_GF = 0, 8, 16, 24
C_CW, C_CB = 32, 36
C_BAF, C_BXF, C_LAMF, C_BAB, C_BXB, C_LAMB = 37, 38, 39, 40, 41, 42
C_FB = 45
C_GL, C_GFO = 46, 50

TILES5 = [(0, 512), (512, 512), (1024, 512), (1536, 512), (2048, 16)]
TILES4 = TILES5[:4]


class _Dummy:
    def then_inc(self, *a, **kw):
        return self


class _Proxy:
    def __init__(self, real, k):
        self._real, self._k = real, k

    def __getattr__(self, name):
        attr = getattr(self._real, name)
        if not callable(attr):
            return attr

        def f(*a, **kw):
            if self._k.stopped:
                return _Dummy()
            return attr(*a, **kw)
        return f


class K:
    def __init__(self, nc, es):
        self.nc, self.es = nc, es
        self.stopped = False
        self.eng = {"pe": nc.tensor, "act": nc.scalar, "dve": nc.vector, "pool": nc.gpsimd, "sp": nc.sync}
        self.eng = {n: _Proxy(e, self) for n, e in self.eng.items()}
        self.sem = {k: es.enter_context(nc.semaphore("sem_" + k)) for k in self.eng}
        self.cnt = {k: 0 for k in self.eng}
        self.seen = {}
        self.dsem, self.dcnt = {}, {}

    def sig(self, e, ins):
        ins.then_inc(self.sem[e], 1)
        self.cnt[e] += 1
        return ("e", e, self.cnt[e])

    def sw(self, e, ins):
        tok = self.sig(e, ins)
        self.eng[e].wait_ge(self.sem[e], tok[2])
        return tok

    def wait(self, e, *toks):
        for t in toks:
            if t is None:
                continue
            if isinstance(t, list):
                self.wait(e, *t)
                continue
            if t[0] == "e":
                _, p, v = t
                if p == e:
                    continue
                key = (e, p)
                if self.seen.get(key, 0) >= v:
                    continue
                self.eng[e].wait_ge(self.sem[p], v)
                self.seen[key] = v
            else:
                _, name, v = t
                key = (e, "d_" + name)
                if self.seen.get(key, 0) >= v:
                    continue
                self.eng[e].wait_ge(self.dsem[name], v)
                self.seen[key] = v

    def _dsem(self, name):
        if name not in self.dsem:
            self.dsem[name] = self.es.enter_context(self.nc.semaphore("d_" + name))
            self.dcnt[name] = 0
        return self.dsem[name]

    def dma(self, e, name, out, in_, **kw):
        s = self._dsem(name)
        self.eng[e].dma_start(out=out, in_=in_, **kw).then_inc(s, 16)
        self.dcnt[name] += 16
        return ("d", name, self.dcnt[name])

    def cc(self, name, ins, outs):
        s = self._dsem(name)
        self.eng["pool"].collective_compute("AllGather", ALU.bypass, replica_groups=GROUPS,
                                          ins=[ins], outs=[outs], dma_qos="P3").then_inc(s, 1)
        self.dcnt[name] += 1
        return ("d", name, self.dcnt[name])

    def gather(self, name, out, in_, idx_ap):
        s = self._dsem(name)
        self.eng["pool"].indirect_dma_start(out=out, out_offset=None, in_=in_,
                                          in_offset=bass.IndirectOffsetOnAxis(ap=idx_ap, axis=0)).then_inc(s, 16)
        self.dcnt[name] += 16
        return ("d", name, self.dcnt[name])


class _Stop(Exception):
    pass


def build_nc(debug=False, stop=None):
    nc = bass.Bass("TRN2", target_bir_lowering=False)
    dt = nc.dram_tensor
    xT = dt("xT", [D, TOK], F32, kind="ExternalInput").ap()
    metaT = dt("metaT", [D, NM], F32, kind="ExternalInput").ap()
    w1i = dt("w1i", [NF, 128, 2048], F32, kind="ExternalInput").ap()
    w1o = dt("w1o", [16, 128, 1408], F32, kind="ExternalInput").ap()
    w2i = dt("w2i", [NF, 128, 2048], F32, kind="ExternalInput").ap()
    w2o = dt("w2o", [16, 128, 1408], F32, kind="ExternalInput").ap()
    wsl = dt("wsl", [128, 3072], F32, kind="ExternalInput").ap()
    wop = dt("wop", [8, 128, 1024], F32, kind="ExternalInput").ap()
    pvec = dt("pvec", [128, NV], F32, kind="ExternalInput").ap()
    lruw = dt("lruw", [128, 512], F32, kind="ExternalInput").ap()
    fwbd = dt("fwbd", [128, 128], F32, kind="ExternalInput").ap()
    c64 = dt("c64", [128, 256], F32, kind="ExternalInput").ap()
    ident = dt("ident", [128, 128], F32, kind="ExternalInput").ap()
    c1s1 = dt("c1s1", [N1, 432], F32, kind="ExternalInput").ap()
    gtab = dt("gtab", [N2, N1 * 152], F32, kind="ExternalInput").ap()
    idx = dt("idx", [128, 32], U32, kind="ExternalInput").ap()
    outT = dt("outT", [D, TOK], F32, kind="ExternalOutput").ap()
    dbg = {}
    if debug:
        dbg["h1"] = dt("dbg_h1", [128, 8 * NCOL], F32, kind="ExternalOutput").ap()
        dbg["xc"] = dt("dbg_xc", [128, T], F32, kind="ExternalOutput").ap()
        dbg["gt"] = dt("dbg_gt", [128, T], F32, kind="ExternalOutput").ap()
        dbg["hf"] = dt("dbg_hf", [128, T], F32, kind="ExternalOutput").ap()
        dbg["yl"] = dt("dbg_yl", [128, T], F32, kind="ExternalOutput").ap()
        dbg["yf"] = dt("dbg_yf", [128, T], F32, kind="ExternalOutput").ap()

    HS = dt("hs", [128, 8 * NCOL], F32, kind="Internal").ap()
    HB = [dt(f"hb{j}", [1024, 512], BF16, kind="Internal").ap() for j in range(4)]
    HG = [dt(f"hg{j}", [4096, 512], BF16, kind="Internal").ap() for j in range(4)]
    YB = [dt(f"yb{i}", [512, 512], F32, kind="Internal").ap() for i in range(8)]
    GB = dt("gb", [16384, 512], F32, kind="Internal").ap()

    with ExitStack() as es:
        k = K(nc, es)
        pe, act, dve, pool, sp = (k.eng[n] for n in ("pe", "act", "dve", "pool", "sp"))

        def sb(stack, name, shape, dtype):
            return stack.enter_context(nc.sbuf_tensor(name, shape, dtype))

        PS = [es.enter_context(nc.psum_tensor(f"ps{i}", [128, 512], F32)) for i in range(8)]
        psf = [None] * 8
        PV = sb(es, "pv", [128, NV], F32)
        ONES = sb(es, "ones", [128, 128], BF16)
        IDENT = sb(es, "identsb", [128, 128], F32)
        EPSC = sb(es, "epsc", [128, 1], F32)
        RS = [sb(es, f"rs{i}", [128, 512], F32) for i in range(2)]
        SQ = sb(es, "sq", [128, 8, 512], BF16)
        HNM = sb(es, "hnm", [128, 8, NM], BF16)
        IDX = sb(es, "idxsb", [128, 32], U32)
        block = es.enter_context(nc.Block())

        st = {"rsf": [None, None], "sqf": None}
        t_const = []

        def emit_norm(src, dst, gcol, tiles, kgroups, src_ready, dst_free):
            toks = []
            for ti, (c0, n) in enumerate(tiles):
                for kcs in kgroups:
                    ri = emit_norm.cnt % 2
                    emit_norm.cnt += 1
                    k.wait("act", src_ready, st["sqf"])
                    for kc in kcs:
                        a = act.activation(out=SQ[:, kc, 0:n], in_=src[:, kc, c0:c0 + n], func=AF.Square)
                    t_sq = k.sig("act", a)
                    k.wait("pe", t_sq, t_const, psf[6])
                    for i, kc in enumerate(kcs):
                        mm = pe.matmul(PS[6][:, 0:n], lhsT=ONES[:], rhs=SQ[:, kc, 0:n],
                                       start=(i == 0), stop=(i == len(kcs) - 1))
                    t_mm = k.sig("pe", mm)
                    st["sqf"] = t_mm
                    k.wait("act", t_mm, st["rsf"][ri])
                    a = act.activation(out=RS[ri][:, 0:n], in_=PS[6][:, 0:n], func=AF.Sqrt,
                                       bias=EPSC[:, 0:1], scale=1.0 / (128 * len(kcs)))
                    t_a = k.sig("act", a)
                    psf[6] = t_a
                    k.wait("dve", t_a, dst_free)
                    k.sw("dve", dve.reciprocal(out=RS[ri][:, 0:n], in_=RS[ri][:, 0:n]))
                    for kc in kcs:
                        d = dve.scalar_tensor_tensor(out=dst[:, kc, c0:c0 + n], in0=src[:, kc, c0:c0 + n],
                                                     scalar=PV[:, gcol + kc:gcol + kc + 1], in1=RS[ri][:, 0:n],
                                                     op0=ALU.mult, op1=ALU.mult)
                    t_d = k.sig("dve", d)
                    st["rsf"][ri] = t_d
                toks.append(t_d)
            return toks
        emit_norm.cnt = 0

        GW = 1040
        NSLOT = 5

        def alloc_ffn(stack, tag):
            fb = {"G": sb(stack, "g" + tag, [128, 11, GW], BF16),
                  "WI": [sb(stack, f"wi{tag}{i}", [128, 8, 256], BF16) for i in range(NSLOT)],
                  "WO": [sb(stack, f"wo{tag}{i}", [128, 11, 128], BF16) for i in range(NSLOT)],
                  "SS": [sb(stack, f"ss{tag}{i}", [128, 512], F32) for i in range(2)],
                  "wif": [None] * NSLOT, "wof": [None] * NSLOT, "ssf": [None, None],
                  "cnt": {"i": 0, "o": 0, "g": 0, "p": 0}, "g_free": None, "pre": {}}
            return fb

        FFN_JOBS = []
        for half in range(2):
            FFN_JOBS += [("i", half * 11 + jj, jj) for jj in range(11)]
            FFN_JOBS += [("o", half * 8 + i, i) for i in range(8)]

        def ffn_issue(fb, loaded, n, wi_d, wo_d):
            if n >= len(FFN_JOBS) or n in loaded:
                return
            kind, a, _ = FFN_JOBS[n]
            c = fb["cnt"][kind]
            fb["cnt"][kind] += 1
            slot = c % NSLOT
            if kind == "i":
                k.wait("pool", fb["wif"][slot])
                loaded[n] = (slot, k.dma("pool", f"wi{slot}", fb["WI"][slot][:].rearrange("p a b -> p (a b)"), wi_d[a],
                                          max_dma_last_dim=4096))
            else:
                k.wait("pool", fb["wof"][slot])
                loaded[n] = (slot, k.dma("pool", f"wo{slot}", fb["WO"][slot][:].rearrange("p a b -> p (a b)"), wo_d[a],
                                          max_dma_last_dim=4096))

        def ffn_prefetch(fb, wi_d, wo_d):
            for n in range(NSLOT - 1):
                ffn_issue(fb, fb["pre"], n, wi_d, wo_d)

        def emit_ffn(fb, H, HN, wi_d, wo_d, tiles, hn_ready):
            G, WI, WO, SS = fb["G"], fb["WI"], fb["WO"], fb["SS"]
            wif, wof, ssf, cnt = fb["wif"], fb["wof"], fb["ssf"], fb["cnt"]
            loaded = fb["pre"]
            fb["pre"] = {}
            base = tiles[0][0]
            for n in range(NSLOT - 1):
                ffn_issue(fb, loaded, n, wi_d, wo_d)
            g_ready = None
            g_free = fb["g_free"]
            h_tok = None
            for n, (kind, a, b) in enumerate(FFN_JOBS):
                ffn_issue(fb, loaded, n + NSLOT - 1, wi_d, wo_d)
                slot, t_w = loaded[n]
                if kind == "i":
                    jj = b
                    for (c0, nn) in tiles:
                        pb = cnt["g"] % 2
                        cnt["g"] += 1
                        PG, PU = PS[pb], PS[2 + pb]
                        k.wait("pe", t_w, hn_ready, psf[pb], psf[2 + pb])
                        for kc in range(8):
                            pe.matmul(PG[:, 0:nn], lhsT=WI[slot][:, kc, 0:128], rhs=HN[:, kc, c0:c0 + nn],
                                      start=(kc == 0), stop=(kc == 7))
                        for kc in range(8):
                            mm = pe.matmul(PU[:, 0:nn], lhsT=WI[slot][:, kc, 128:256], rhs=HN[:, kc, c0:c0 + nn],
                                           start=(kc == 0), stop=(kc == 7))
                        t_mm = k.sig("pe", mm)
                        k.wait("act", t_mm, ssf[pb])
                        ai = act.activation(out=SS[pb][:, 0:nn], in_=PG[:, 0:nn], func=AF.Silu)
                        t_a = k.sig("act", ai)
                        k.wait("dve", t_a, g_free)
                        d = dve.tensor_tensor(out=G[:, jj, c0 - base:c0 - base + nn], in0=SS[pb][:, 0:nn],
                                              in1=PU[:, 0:nn], op=ALU.mult)
                        t_d = k.sig("dve", d)
                        psf[pb] = t_d
                        psf[2 + pb] = t_d
                        ssf[pb] = t_d
                    wif[slot] = t_mm
                    g_ready = t_d
                else:
                    i = b
                    for (c0, nn) in tiles:
                        pb = 4 + cnt["p"] % 2
                        cnt["p"] += 1
                        k.wait("pe", t_w, g_ready, psf[pb])
                        for fc in range(11):
                            mm = pe.matmul(PS[pb][:, 0:nn], lhsT=WO[slot][:, fc, :],
                                           rhs=G[:, fc, c0 - base:c0 - base + nn],
                                           start=(fc == 0), stop=(fc == 10))
                        t_mm = k.sig("pe", mm)
                        k.wait("dve", t_mm)
                        d = dve.scalar_tensor_tensor(out=H[:, i, c0:c0 + nn], in0=PS[pb][:, 0:nn], scalar=0.5,
                                                     in1=H[:, i, c0:c0 + nn], op0=ALU.mult, op1=ALU.add)
                        t_d = k.sig("dve", d)
                        psf[pb] = t_d
                    wof[slot] = t_mm
                    g_free = t_mm
                    h_tok = t_d
            fb["g_free"] = g_free
            return h_tok

        def emit_norm_rel(src, dst, c0, n, src_ready):
            t_d = None
            for gi, (kcs, gcol) in enumerate([(list(range(4)), C_GL), (list(range(4, 8)), C_GFO - 4)]):
                ri = emit_norm.cnt % 2
                emit_norm.cnt += 1
                k.wait("act", src_ready, st["sqf"])
                for kc in kcs:
                    a = act.activation(out=SQ[:, kc, 0:n], in_=src[:, kc, 0:n], func=AF.Square)
                t_sq = k.sig("act", a)
                k.wait("pe", t_sq, psf[6])
                for i, kc in enumerate(kcs):
                    mm = pe.matmul(PS[6][:, 0:n], lhsT=ONES[:], rhs=SQ[:, kc, 0:n], start=(i == 0), stop=(i == 3))
                t_mm = k.sig("pe", mm)
                st["sqf"] = t_mm
                k.wait("act", t_mm, st["rsf"][ri])
                a = act.activation(out=RS[ri][:, 0:n], in_=PS[6][:, 0:n], func=AF.Sqrt, bias=EPSC[:, 0:1],
                                   scale=1.0 / 512.0)
                t_a = k.sig("act", a)
                psf[6] = t_a
                k.wait("dve", t_a)
                k.sw("dve", dve.reciprocal(out=RS[ri][:, 0:n], in_=RS[ri][:, 0:n]))
                for kc in kcs:
                    d = dve.scalar_tensor_tensor(out=dst[:, kc, c0:c0 + n], in0=src[:, kc, 0:n],
                                                 scalar=PV[:, gcol + kc:gcol + kc + 1], in1=RS[ri][:, 0:n],
                                                 op0=ALU.mult, op1=ALU.mult)
                t_d = k.sig("dve", d)
                st["rsf"][ri] = t_d
            return t_d

        def stop_here(name, toks):
            if stop == name:
                for e in ("sp", "pool", "dve", "act", "pe"):
                    k.wait(e, toks)
                k.stopped = True

        def _body(g):
            global_toks = []
            t_pv = k.dma("sp", "pv", PV[:], pvec)
            t_id = k.dma("sp", "ident", IDENT[:], ident)
            t_ix = k.dma("sp", "idx", IDX[:], idx)
            dve.memset(ONES[:], 1.0)
            m = dve.memset(EPSC[:], EPS)
            t_const.extend([k.sig("dve", m), t_pv, t_id])

            with ExitStack() as sa:
                HN = sb(sa, "hn_a", [128, 8, NCOL], BF16)
                HN2 = sb(sa, "hn2_a", [128, 8, 1024], BF16)
                H = sb(sa, "h_a", [128, 8, NCOL], F32)
                fb = alloc_ffn(sa, "a")
                ffn_prefetch(fb, w1i, w1o)
                ld = []
                for kc in range(8):
                    ld.append(k.dma("sp", "xin", H[:, kc, 0:TOK], xT[kc * 128:(kc + 1) * 128, :]))
                    ld.append(k.dma("sp", "min", H[:, kc, TOK:NCOL], metaT[kc * 128:(kc + 1) * 128, :]))
                k.wait("dve", t_pv)
                tn = emit_norm(H, HN, C_G1, TILES5, [list(range(8))], [ld[-1], ld[-2], t_pv], None)
                TA, TB = TILES5[:2], TILES5[2:]
                t_hA = emit_ffn(fb, H, HN, w1i, w1o, TA, tn[1])
                ffn_prefetch(fb, w1i, w1o)
                t_ag, t_bs = [], []
                tnA = emit_norm(H, HN2, C_GM, TA, [list(range(8))], t_hA, None)
                for j in range(2):
                    k.wait("sp", tnA[j])
                    t_b = k.dma("sp", f"hb{j}", HB[j].rearrange("(kc p) c -> p kc c", p=128),
                                HN2[:, :, j * 512:(j + 1) * 512])
                    t_bs.append(t_b)
                    k.wait("pool", t_b)
                    t_ag.append(k.cc(f"agf{j}", HB[j], HG[j]))
                t_h = emit_ffn(fb, H, HN, w1i, w1o, TB, tn[-1])
                if debug:
                    k.wait("sp", t_h)
                    global_toks.append(k.dma("sp", "dbg", dbg["h1"], H[:].rearrange("p a b -> p (a b)")))
                stop_here("ffn1", [t_h] + global_toks + t_bs + t_ag)
                tnB = emit_norm(H, HN, C_GM, TB, [list(range(8))], t_h, t_h)
                k.wait("sp", t_h)
                t_spill = k.dma("sp", "hs", HS, H[:].rearrange("p a b -> p (a b)"))
                k.wait("pool", tnB[2])
                cp = pool.tensor_copy(out=HNM[:], in_=HN[:, :, TOK:NCOL])
                t_hnm = k.sig("pool", cp)
                for j in range(2, 4):
                    k.wait("sp", tnB[j - 2])
                    t_b = k.dma("sp", f"hb{j}", HB[j].rearrange("(kc p) c -> p kc c", p=128),
                                HN[:, :, j * 512:(j + 1) * 512])
                    t_bs.append(t_b)
                    k.wait("pool", t_b)
                    t_ag.append(k.cc(f"agf{j}", HB[j], HG[j]))
                t_a_done = [t_spill, t_hnm] + t_bs + global_toks
                stop_here("agf", t_a_done + t_ag)
                for e in ("sp", "pool", "dve", "act", "pe"):
                    k.wait(e, t_a_done)

            with ExitStack() as sbk:
                V = sb(sbk, "v", [128, T], BF16)
                with ExitStack() as sl:
                    GT = sb(sl, "gt", [128, T], F32)
                    XC = sb(sl, "xc", [128, T], F32)
                    XCB = sb(sl, "xcb", [128, T], BF16)
                    sx = ExitStack()
                    XL = sb(sx, "xl", [128, T + 3], F32)
                    m1 = pool.memset(XL[:, 0:2], 0.0)
                    m1 = pool.memset(XL[:, T + 2:T + 3], 0.0)
                    t_pad = k.sig("pool", m1)
                    with ExitStack() as s1:
                        WSL = sb(s1, "wslsb", [128, 8, 384], BF16)
                        HNA = [sb(s1, f"hna{i}", [128, 8, 512], BF16) for i in range(4)]
                        t_wsl = k.dma("pool", "wsl", WSL[:].rearrange("p a b -> p (a b)"), wsl, max_dma_last_dim=4096)
                        hnaf = [None] * 4
                        cntb = 0
                        nload = 0
                        t_last_act = None

                        def inproj(rhs_of_kc, n, pos, t_in):
                            nonlocal cntb, t_last_act
                            t_mm = None
                            for c in range(3):
                                pb = 4 + cntb % 2
                                cntb += 1
                                k.wait("pe", t_in, t_wsl, psf[pb])
                                for kc in range(8):
                                    mm = pe.matmul(PS[pb][:, 0:n], lhsT=WSL[:, kc, c * 128:(c + 1) * 128],
                                                   rhs=rhs_of_kc(kc), start=(kc == 0), stop=(kc == 7))
                                t_mm = k.sig("pe", mm)
                                k.wait("act", t_mm, t_pad)
                                dest = [XL[:, 2 + pos:2 + pos + n], GT[:, pos:pos + n], V[:, pos:pos + n]][c]
                                a = act.activation(out=dest, in_=PS[pb][:, 0:n], func=AF.Copy)
                                t_last_act = k.sig("act", a)
                                psf[pb] = t_last_act
                            return t_mm

                        inproj(lambda kc: HNM[:, kc, :], NM, 0, t_hnm)
                        for j in range(4):
                            for r in range(4):
                                slot = nload % 4
                                nload += 1
                                k.wait("sp", t_ag[j], hnaf[slot])
                                t_l = k.dma("sp", f"hna{slot}", HNA[slot][:],
                                            HG[j][r * 1024:(r + 1) * 1024, :].rearrange("(kc p) c -> p kc c", p=128))
                                hnaf[slot] = inproj(lambda kc, s=slot: HNA[s][:, kc, :], 512,
                                                    NM + TOK * r + 512 * j, t_l)
                        t_b1 = t_last_act
                        if debug and stop == "b1":
                            k.wait("sp", t_b1)
                            global_toks.append(k.dma("sp", "dbg", dbg["xc"], XL[:, 2:T + 2]))
                            global_toks.append(k.dma("sp", "dbg", dbg["gt"], GT[:, 0:T]))
                        stop_here("b1", [t_b1] + global_toks)
                        k.wait("pe", t_b1)
                        k.wait("pool", t_b1)
                        k.wait("dve", t_b1)
                        k.wait("sp", t_b1)

                    chunks_c = [(c, 2052) for c in range(0, T, 2052)]
                    t_xc = []
                    for (c0, n) in chunks_c:
                        k.wait("pool", t_pv)
                        p0 = pool.tensor_scalar(out=XC[:, c0:c0 + n], in0=XL[:, c0 + 2:c0 + 2 + n],
                                                scalar1=PV[:, C_CW + 2:C_CW + 3], scalar2=PV[:, C_CB:C_CB + 1],
                                                op0=ALU.mult, op1=ALU.add)
                        t_p0 = k.sig("pool", p0)
                        k.wait("dve", t_p0)
                        for kk in (0, 1, 3):
                            d = dve.scalar_tensor_tensor(out=XC[:, c0:c0 + n], in0=XL[:, c0 + kk:c0 + kk + n],
                                                         scalar=PV[:, C_CW + kk:C_CW + kk + 1], in1=XC[:, c0:c0 + n],
                                                         op0=ALU.mult, op1=ALU.add)
                            t_d = k.sw("dve", d)
                        k.wait("pool", t_d)
                        p1 = pool.tensor_copy(out=XCB[:, c0:c0 + n], in_=XC[:, c0:c0 + n])
                        t_xc = [t_d, k.sig("pool", p1)]
                    if debug:
                        k.wait("sp", t_xc)
                        global_toks.append(k.dma("sp", "dbg", dbg["xc"], XC[:, 0:T]))
                        global_toks.append(k.dma("sp", "dbg", dbg["gt"], GT[:, 0:T]))
                    stop_here("conv", t_xc + global_toks)
                    for e in ("sp", "pool", "dve", "act", "pe"):
                        k.wait(e, t_xc, global_toks)
                    sx.close()

                    with ExitStack() as s2:
                        HF = sb(s2, "hf", [128, T], F32)
                        LW = sb(s2, "lw", [128, 4, 128], BF16)
                        CL = sb(s2, "cl", [128, 8], F32)
                        HALF = sb(s2, "half", [128, 1024], F32)
                        RB = [sb(s2, f"rb{i}", [128, 1024], F32) for i in range(2)]
                        IB = [sb(s2, f"ib{i}", [128, 1024], F32) for i in range(2)]
                        AB = [sb(s2, f"ab{i}", [128, 1024], F32) for i in range(2)]
                        TB = [sb(s2, f"tb{i}", [128, 1024], F32) for i in range(2)]
                        t_lw = k.dma("pool", "lw", LW[:].rearrange("p a b -> p (a b)"), lruw)
                        t_half = k.sig("pool", pool.memset(HALF[:], 0.5))
                        k.wait("act", t_pv)
                        act.activation(out=CL[:, 4:5], in_=PV[:, C_LAMF:C_LAMF + 1], func=AF.Exp, scale=-1.0)
                        a = act.activation(out=CL[:, 5:6], in_=PV[:, C_LAMB:C_LAMB + 1], func=AF.Exp, scale=-1.0)
                        t_e = k.sig("act", a)
                        k.wait("dve", t_e)
                        E = CL[:, 4:6]
                        Pp = CL[:, 6:8]
                        k.sw("dve", dve.tensor_scalar(out=Pp, in0=E, scalar1=1.0 / 7.0, scalar2=-1.0 / 6.0,
                                                      op0=ALU.mult, op1=ALU.add))
                        for cst in (1.0 / 5.0, -1.0 / 4.0, 1.0 / 3.0, -0.5, 1.0):
                            k.sw("dve", dve.tensor_tensor(out=Pp, in0=Pp, in1=E, op=ALU.mult))
                            k.sw("dve", dve.tensor_scalar(out=Pp, in0=Pp, scalar1=cst, scalar2=None, op0=ALU.add))
                        k.sw("dve", dve.tensor_tensor(out=Pp, in0=Pp, in1=E, op=ALU.mult))
                        d = dve.tensor_scalar(out=CL[:, 0:2], in0=Pp, scalar1=-8.0, scalar2=None, op0=ALU.mult)
                        t_cl = k.sw("dve", d)
                        stop_here("setup", [t_cl, t_lw, t_half] + global_toks)

                        chunks = [(c, 1024) for c in range(0, SEQ, 1024)] + [(SEQ, NM)]
                        setf = [None, None]
                        GELU_A = math.sqrt(0.044715)

                        def fence(e, ins, cond):
                            return k.sw(e, ins) if cond else k.sig(e, ins)

                        def lru_pass(direction):
                            gi = 0 if direction == 0 else 2
                            cba, cbx = (C_BAF, C_BXF) if direction == 0 else (C_BAB, C_BXB)
                            order = chunks if direction == 0 else chunks[::-1]
                            first = True
                            t_out = None
                            for ci, (c0, n) in enumerate(order):
                                s_ = ci % 2
                                small = n < 128
                                R, I, A, Tt = RB[s_], IB[s_], AB[s_], TB[s_]
                                subs = [(s0, min(512, n - s0)) for s0 in range(0, n, 512)]
                                t_mms = []
                                for si, (s0, ns) in enumerate(subs):
                                    b0, b1 = PS[4 * s_ + si], PS[4 * s_ + 2 + si]
                                    k.wait("pe", t_lw, t_xc, psf[4 * s_ + si], psf[4 * s_ + 2 + si])
                                    pe.matmul(b0[:, 0:ns], lhsT=LW[:, gi, :], rhs=XCB[:, c0 + s0:c0 + s0 + ns],
                                              start=True, stop=True)
                                    mm = pe.matmul(b1[:, 0:ns], lhsT=LW[:, gi + 1, :],
                                                   rhs=XCB[:, c0 + s0:c0 + s0 + ns], start=True, stop=True)
                                    t_mms.append(k.sig("pe", mm))
                                k.wait("act", t_mms, setf[s_], t_cl)
                                for si, (s0, ns) in enumerate(subs):
                                    act.activation(out=R[:, s0:s0 + ns], in_=PS[4 * s_ + si][:, 0:ns], func=AF.Sigmoid,
                                                   bias=PV[:, cba:cba + 1], scale=1.0)
                                    a = act.activation(out=I[:, s0:s0 + ns], in_=PS[4 * s_ + 2 + si][:, 0:ns],
                                                       func=AF.Sigmoid, bias=PV[:, cbx:cbx + 1], scale=1.0)
                                    t_a = fence("act", a, small)
                                    psf[4 * s_ + si] = t_a
                                    psf[4 * s_ + 2 + si] = t_a
                                t_gs = None
                                if direction == 0:
                                    gt = GT[:, c0:c0 + n]
                                    a = act.activation(out=Tt[:, 0:n], in_=gt, func=AF.Square, scale=GELU_A)
                                    t_q = k.sig("act", a)
                                    k.wait("dve", t_q)
                                    d = dve.scalar_tensor_tensor(out=Tt[:, 0:n], in0=Tt[:, 0:n], scalar=1.0, in1=gt,
                                                                 op0=ALU.add, op1=ALU.mult)
                                    t_u = k.sig("dve", d)
                                    k.wait("act", t_u)
                                    a = act.activation(out=Tt[:, 0:n], in_=Tt[:, 0:n], func=AF.Sigmoid,
                                                       scale=1.5957691216057308)
                                    t_s = k.sig("act", a)
                                    k.wait("pool", t_s)
                                    t_gs = k.sig("pool", pool.tensor_tensor(out=gt, in0=gt, in1=Tt[:, 0:n], op=ALU.mult))
                                a = act.activation(out=A[:, 0:n], in_=R[:, 0:n], func=AF.Exp,
                                                   scale=CL[:, direction:direction + 1])
                                fence("act", a, small)
                                k.wait("act", t_gs)
                                a = act.activation(out=Tt[:, 0:n], in_=A[:, 0:n], func=AF.Square)
                                t_act = k.sig("act", a)
                                k.wait("pool", t_act, t_half)
                                p_ = pool.tensor_scalar(out=Tt[:, 0:n], in0=Tt[:, 0:n], scalar1=1.0, scalar2=-1.0,
                                                        op0=ALU.min, op1=ALU.mult)
                                fence("pool", p_, small)
                                p_ = pool.tensor_scalar(out=Tt[:, 0:n], in0=Tt[:, 0:n], scalar1=1.0, scalar2=None,
                                                        op0=ALU.add)
                                fence("pool", p_, small)
                                p_ = pool.tensor_tensor(out=Tt[:, 0:n], in0=Tt[:, 0:n], in1=HALF[:, 0:n], op=ALU.pow)
                                t_pool = k.sig("pool", p_)
                                k.wait("dve", t_act, t_pool)
                                fence("dve", dve.tensor_tensor(out=I[:, 0:n], in0=I[:, 0:n], in1=Tt[:, 0:n], op=ALU.mult),
                                      small)
                                d = dve.tensor_tensor(out=I[:, 0:n], in0=I[:, 0:n], in1=XC[:, c0:c0 + n], op=ALU.mult)
                                fence("dve", d, small or direction == 1)
                                if direction == 0:
                                    init = 0.0 if first else HF[:, c0 - 1:c0]
                                    d = dve.tensor_tensor_scan(out=HF[:, c0:c0 + n], data0=A[:, 0:n], data1=I[:, 0:n],
                                                               initial=init, op0=ALU.mult, op1=ALU.add)
                                    t_out = k.sw("dve", d)
                                else:
                                    init = 0.0 if first else CL[:, 4:5]
                                    k.sw("dve", dve.tensor_tensor_scan(out=R[:, 0:n][:, ::-1], data0=A[:, 0:n][:, ::-1],
                                                                       data1=I[:, 0:n][:, ::-1], initial=init,
                                                                       op0=ALU.mult, op1=ALU.add))
                                    k.sw("dve", dve.tensor_copy(out=CL[:, 4:5], in_=R[:, 0:1]))
                                    fence("dve", dve.tensor_tensor(out=R[:, 0:n], in0=R[:, 0:n], in1=HF[:, c0:c0 + n],
                                                                   op=ALU.add), small)
                                    d = dve.tensor_tensor(out=GT[:, c0:c0 + n], in0=R[:, 0:n], in1=GT[:, c0:c0 + n],
                                                          op=ALU.mult)
                                    t_out = k.sig("dve", d)
                                setf[s_] = t_out
                                first = False
                            return t_out

                        t_f = lru_pass(0)
                        if debug:
                            k.wait("sp", t_f)
                            global_toks.append(k.dma("sp", "dbg", dbg["hf"], HF[:, 0:T]))
                        stop_here("lruf", [t_f] + global_toks)
                        t_yl = lru_pass(1)
                        if debug:
                            k.wait("sp", t_yl)
                            global_toks.append(k.dma("sp", "dbg", dbg["yl"], GT[:, 0:T]))
                        stop_here("lrub", [t_yl] + global_toks)
                        t_agl = {}
                        lru_bounce = {}
                        for hh in range(2):
                            for dp in range(2):
                                u = (hh * 2 + 0) * 2 + dp
                                k.wait("sp", t_yl)
                                for e_ in range(2):
                                    a0 = NM + TOK * (2 * dp + e_) + 1024 * hh
                                    t_b = k.dma("sp", f"yb{u}",
                                                YB[u][e_ * 256:(e_ + 1) * 256, :].rearrange("(t p) c -> p t c", p=128),
                                                GT[:, a0:a0 + 1024].rearrange("p (t c) -> p t c", c=512))
                                lru_bounce[u] = t_b
                                if hh == 0:
                                    k.wait("pool", t_b)
                                    t_agl[u] = k.cc(f"agb{u}", YB[u], GB[u * 2048:(u + 1) * 2048, :])
                        t_l_done = [t_yl] + global_toks + list(lru_bounce.values())
                        for e in ("sp", "pool", "dve", "act", "pe"):
                            k.wait(e, t_l_done)

                with ExitStack() as s3:
                    Z = sb(s3, "z", [N1, N2, 256], BF16)
                    AS = sb(s3, "as", [N2, 128, 216], BF16)
                    GTB = sb(s3, "gtb", [N2, N1, 152], BF16)
                    YF = sb(s3, "yf", [128, N2, N1], F32)
                    C1 = sb(s3, "c1", [N1, 2, 216], BF16)
                    C64 = sb(s3, "c64sb", [128, 2, 128], F32)
                    FW = sb(s3, "fwsb", [128, 128], F32)
                    MB = sb(s3, "mb", [128, 256], BF16)
                    t_c64 = k.dma("sp", "c64", C64[:].rearrange("p a b -> p (a b)"), c64)
                    t_fw = k.dma("sp", "fw", FW[:], fwbd)
                    t_c1 = k.dma("pool", "c1", C1[:].rearrange("p a b -> p (a b)"), c1s1)
                    t_gt = None
                    for q4 in range(4):
                        t_gt = k.dma("pool", "gtb", GTB[:, q4 * 27:(q4 + 1) * 27, :],
                                     gtab.rearrange("p (a b) -> p a b", b=152)[:, q4 * 27:(q4 + 1) * 27, :])
                    CH = sb(s3, "c64h", [128, 2, 128], BF16)
                    CLo = sb(s3, "c64l", [128, 2, 128], BF16)
                    WH = sb(s3, "fwh", [128, 128], BF16)
                    WLo = sb(s3, "fwl", [128, 128], BF16)
                    k.wait("dve", t_c64, t_fw)
                    k.sw("dve", dve.tensor_copy(out=CH[:], in_=C64[:]))
                    k.sw("dve", dve.tensor_copy(out=WH[:], in_=FW[:]))
                    k.sw("dve", dve.tensor_tensor(out=CLo[:], in0=C64[:], in1=CH[:], op=ALU.subtract))
                    t_hl = k.sw("dve", dve.tensor_tensor(out=WLo[:], in0=FW[:], in1=WH[:], op=ALU.subtract))
                    k.wait("pe", t_hl, psf[0])
                    for u in range(2):
                        pe.matmul(PS[0][:, u * 128:(u + 1) * 128], lhsT=CH[:, u, :], rhs=WH[:], start=True, stop=False)
                        pe.matmul(PS[0][:, u * 128:(u + 1) * 128], lhsT=CH[:, u, :], rhs=WLo[:], start=False, stop=False)
                        mm = pe.matmul(PS[0][:, u * 128:(u + 1) * 128], lhsT=CLo[:, u, :], rhs=WH[:], start=False, stop=True)
                    t_mm = k.sig("pe", mm)
                    k.wait("act", t_mm)
                    a = act.activation(out=MB[:], in_=PS[0][:, 0:256], func=AF.Copy)
                    t_mb = k.sig("act", a)
                    psf[0] = t_mb
                    cn = 0
                    t_z = None
                    for t2p in range(0, N2, 2):
                        pb = cn % 4
                        cn += 1
                        k.wait("pe", t_mb, psf[pb])
                        for u in range(2):
                            t2 = t2p + u
                            mm = pe.matmul(PS[pb][0:N1, u * 256:(u + 1) * 256], lhsT=V[:, t2:T:N2], rhs=MB[:],
                                           start=True, stop=True)
                        t_mm = k.sig("pe", mm)
                        eng = "act" if (cn % 2) else "dve"
                        k.wait(eng, t_mm)
                        dst = Z[:, t2p:t2p + 2, :].rearrange("p a b -> p (a b)")
                        if eng == "act":
                            i_ = act.activation(out=dst, in_=PS[pb][0:N1, 0:512], func=AF.Copy)
                        else:
                            i_ = dve.tensor_copy(out=dst, in_=PS[pb][0:N1, 0:512])
                        t_z0 = k.sig(eng, i_)
                        psf[pb] = t_z0
                        t_z = [t_z0] if t_z is None else (t_z + [t_z0])[-2:]
                    t_as = None
                    for jp in range(0, 128, 2):
                        pb = cn % 4
                        cn += 1
                        k.wait("pe", t_z, t_c1, psf[pb])
                        for u in range(2):
                            j = jp + u
                            pe.matmul(PS[pb][0:N2, u * 216:(u + 1) * 216], lhsT=Z[:, :, j], rhs=C1[:, 0, :],
                                      start=True, stop=False)
                            mm = pe.matmul(PS[pb][0:N2, u * 216:(u + 1) * 216], lhsT=Z[:, :, 128 + j], rhs=C1[:, 1, :],
                                           start=False, stop=True)
                        t_mm = k.sig("pe", mm)
                        eng = "act" if (cn % 2) else "dve"
                        k.wait(eng, t_mm)
                        dst = AS[:, jp:jp + 2, :].rearrange("p a b -> p (a b)")
                        if eng == "act":
                            i_ = act.activation(out=dst, in_=PS[pb][0:N2, 0:432], func=AF.Copy)
                        else:
                            i_ = dve.tensor_copy(out=dst, in_=PS[pb][0:N2, 0:432])
                        t_a0 = k.sig(eng, i_)
                        psf[pb] = t_a0
                        t_as = [t_a0] if t_as is None else (t_as + [t_a0])[-2:]
                    t_yf = None
                    for kg in range(0, N1, 6):
                        pb = cn % 4
                        cn += 1
                        k.wait("pe", t_as, t_gt, psf[pb])
                        for u in range(6):
                            k1 = kg + u
                            pe.matmul(PS[pb][:, u * N2:(u + 1) * N2], lhsT=AS[:, :, k1], rhs=GTB[:, k1, 0:N2],
                                      start=True, stop=False)
                            mm = pe.matmul(PS[pb][:, u * N2:(u + 1) * N2], lhsT=AS[:, :, N1 + k1],
                                           rhs=GTB[:, k1, N2:2 * N2], start=False, stop=True)
                        t_mm = k.sig("pe", mm)
                        k.wait("dve", t_mm, t_pv)
                        i_ = dve.tensor_scalar(out=YF[:, :, kg:kg + 6],
                                               in0=PS[pb][:, 0:6 * N2].rearrange("p (a b) -> p b a", b=N2),
                                               scalar1=PV[:, C_FB:C_FB + 1], scalar2=None, op0=ALU.add)
                        t_y0 = k.sig("dve", i_)
                        psf[pb] = t_y0
                        t_yf = t_y0
                    YFf = YF[:].rearrange("p a b -> p (a b)")
                    dbg_t = []
                    if debug:
                        k.wait("sp", t_yf)
                        dbg_t.append(k.dma("sp", "dbg", dbg["yf"], YFf[:, 0:T]))
                    stop_here("dft", [t_yf] + dbg_t + global_toks)
                    t_agf = {}
                    four_bounce = {}
                    for hh in range(2):
                        for dp in range(2):
                            u = (hh * 2 + 1) * 2 + dp
                            k.wait("sp", t_yf)
                            for e_ in range(2):
                                a0 = NM + TOK * (2 * dp + e_) + 1024 * hh
                                t_b = k.dma("sp", f"yb{u}",
                                            YB[u][e_ * 256:(e_ + 1) * 256, :].rearrange("(t p) c -> p t c", p=128),
                                            YFf[:, a0:a0 + 1024].rearrange("p (t c) -> p t c", c=512))
                            four_bounce[u] = t_b
                    for hh in range(2):
                        for dp in range(2):
                            u = (hh * 2 + 1) * 2 + dp
                            k.wait("pool", four_bounce[u])
                            t_agf[u] = k.cc(f"agb{u}", YB[u], GB[u * 2048:(u + 1) * 2048, :])
                        if hh == 0:
                            for dp in range(2):
                                u = (1 * 2 + 0) * 2 + dp
                                k.wait("pool", lru_bounce[u])
                                t_agl[u] = k.cc(f"agb{u}", YB[u], GB[u * 2048:(u + 1) * 2048, :])
                    t_f_done = [t_yf] + dbg_t + list(four_bounce.values())
                    for e in ("sp", "pool", "dve", "act", "pe"):
                        k.wait(e, t_f_done)

            t_all_back = list(t_agl.values()) + list(t_agf.values())
            stop_here("back", t_all_back + global_toks)
            with ExitStack() as sc:
                HN = sb(sc, "hn_c", [128, 8, NCOL], BF16)
                H = sb(sc, "h_c", [128, 8, NCOL], F32)
                t_hl = k.dma("sp", "hld", H[:].rearrange("p a b -> p (a b)"), HS)
                YT = [sb(sc, f"yt{i}", [128, 8, 512], F32) for i in range(1)]
                WP = [sb(sc, f"wp{i}", [128, 8, 128], BF16) for i in range(3)]
                fb = alloc_ffn(sc, "c")
                ytf = [None, None]
                wpf = [None] * 3
                cnp = 0
                nwp = 0
                outs = []
                for hh in range(2):
                    tiles = TILES4[2 * hh:2 * hh + 2]
                    t_units = [t_agl[(hh * 2 + 0) * 2 + dp] for dp in range(2)] + \
                              [t_agf[(hh * 2 + 1) * 2 + dp] for dp in range(2)]
                    tns = {}
                    for ti in (2 * hh, 2 * hh + 1):
                        c0, n = TILES4[ti]
                        yb_ = 0
                        k.wait("pool", t_units, ytf[yb_], t_ix)
                        tg = None
                        for kc in range(8):
                            tg = k.gather(f"yt{yb_}", YT[yb_][:, kc, :], GB, IDX[:, ti * 8 + kc:ti * 8 + kc + 1])
                        toks = emit_norm_rel(YT[yb_], HN, c0, n, tg)
                        ytf[yb_] = toks
                        tns[ti] = toks
                    ffn_prefetch(fb, w2i, w2o)
                    t_h2 = None
                    t_mm = None
                    for i in range(8):
                        slot = nwp % 3
                        nwp += 1
                        k.wait("pool", wpf[slot])
                        t_w = k.dma("pool", f"wp{slot}", WP[slot][:].rearrange("p a b -> p (a b)"), wop[i])
                        for ti in (2 * hh, 2 * hh + 1):
                            c0, n = TILES4[ti]
                            pb = 4 + cnp % 2
                            cnp += 1
                            k.wait("pe", t_w, tns[ti], psf[pb])
                            for kc in range(8):
                                mm = pe.matmul(PS[pb][:, 0:n], lhsT=WP[slot][:, kc, :], rhs=HN[:, kc, c0:c0 + n],
                                               start=(kc == 0), stop=(kc == 7))
                            t_mm = k.sig("pe", mm)
                            k.wait("dve", t_mm, t_hl)
                            d = dve.tensor_tensor(out=H[:, i, c0:c0 + n], in0=PS[pb][:, 0:n], in1=H[:, i, c0:c0 + n],
                                                  op=ALU.add)
                            t_h2 = k.sig("dve", d)
                            psf[pb] = t_h2
                        wpf[slot] = t_mm
                    tn = emit_norm(H, HN, C_G2, tiles, [list(range(8))], t_h2, t_mm)
                    t_h = emit_ffn(fb, H, HN, w2i, w2o, tiles, tn[-1])
                    tnf = emit_norm(H, H, C_GF, tiles, [list(range(8))], t_h, None)
                    ca = tiles[0][0]
                    for kc in range(8):
                        k.wait("sp", tnf[-1])
                        outs.append(k.dma("sp", "out", outT[kc * 128:(kc + 1) * 128, ca:ca + 1024], H[:, kc, ca:ca + 1024]))
                for e in ("sp", "pool", "dve", "act", "pe"):
                    k.wait(e, outs[-1], global_toks)

        @block.gpsimd
        def _(g):
            _body(g)

    return nc


def _prep_inputs(inp):
    f32 = np.float32
    x = np.asarray(inp["x"], f32)
    meta = np.asarray(inp["meta_tokens"], f32)

    def lay_in(w):
        w = np.asarray(w, f32).reshape(8, 128, 2, NF, 128)
        return np.ascontiguousarray(w.transpose(3, 1, 0, 2, 4)).reshape(NF, 128, 2048)

    def lay_out(w):
        w = np.asarray(w, f32).reshape(2, 11, 128, 8, 128)
        return np.ascontiguousarray(w.transpose(0, 3, 2, 1, 4)).reshape(16, 128, 1408)

    w1i, w1o = lay_in(inp["w_ffn1_in"][0]), lay_out(inp["w_ffn1_out"][0])
    w2i, w2o = lay_in(inp["w_ffn2_in"][0]), lay_out(inp["w_ffn2_out"][0])
    w_in = np.asarray(inp["w_in"][0], f32)
    w_out = np.asarray(inp["w_out"][0], f32)
    wop = np.ascontiguousarray(w_out.reshape(8, 128, 8, 128).transpose(2, 1, 0, 3)).reshape(8, 128, 1024)

    sc = 1.0 / math.sqrt(64.0 * T)
    cc = np.arange(64)
    ang = 2.0 * np.pi * np.outer(cc, cc) / 64.0
    c64 = np.zeros((128, 2, 128), np.float64)
    for g in range(2):
        c64[g * 64:(g + 1) * 64, 0, g * 64:(g + 1) * 64] = np.cos(ang) * sc
        c64[g * 64:(g + 1) * 64, 1, g * 64:(g + 1) * 64] = -np.sin(ang) * sc
    c64 = c64.reshape(128, 256).astype(f32)
    t1 = np.arange(N1)
    a1 = 2.0 * np.pi * np.outer(t1, t1) / N1
    c1s1 = np.zeros((N1, 2, 216), np.float64)
    c1s1[:, 0, :N1], c1s1[:, 0, N1:] = np.cos(a1), -np.sin(a1)
    c1s1[:, 1, :N1], c1s1[:, 1, N1:] = np.sin(a1), np.cos(a1)
    c1s1 = c1s1.reshape(N1, 432).astype(f32)
    t2 = np.arange(N2)[:, None, None]
    k1 = np.arange(N1)[None, :, None]
    k2 = np.arange(N2)[None, None, :]
    th = 2.0 * np.pi * ((t2 * (k1 + N1 * k2)) % T) / T
    gtab = np.concatenate([np.cos(th), np.sin(th)], axis=2).reshape(N2, N1 * 152).astype(f32)
    ident = np.eye(128, dtype=f32)

    def col8(v):
        return np.asarray(v, f32).reshape(-1, 128).T

    maps = []
    for core in range(8):
        b, q = core // 4, core % 4
        ch = slice(128 * q, 128 * (q + 1))
        xT = np.ascontiguousarray(x[b, TOK * q:TOK * (q + 1), :].T)
        metaT = np.ascontiguousarray(meta.T)
        cols = np.concatenate([np.arange(128 * q, 128 * q + 128), 512 + np.arange(128 * q, 128 * q + 128),
                               1024 + np.arange(128 * q, 128 * q + 128)])
        wsl = np.ascontiguousarray(w_in[:, cols].reshape(8, 128, 384).transpose(1, 0, 2)).reshape(128, 3072)
        pv = np.zeros((128, NV), f32)
        pv[:, C_G1:C_G1 + 8] = col8(inp["norm_ffn1"][0])
        pv[:, C_GM:C_GM + 8] = col8(inp["norm_mix"][0])
        pv[:, C_G2:C_G2 + 8] = col8(inp["norm_ffn2"][0])
        pv[:, C_GF:C_GF + 8] = col8(inp["norm_final"])
        pv[:, C_CW:C_CW + 4] = np.asarray(inp["conv_w"][0], f32)[:, ch].T
        pv[:, C_CB] = np.asarray(inp["conv_b"][0], f32)[ch]
        for c, nm in ((C_BAF, "lru_ba_fwd"), (C_BXF, "lru_bx_fwd"), (C_LAMF, "lru_lambda_fwd"),
                      (C_BAB, "lru_ba_bwd"), (C_BXB, "lru_bx_bwd"), (C_LAMB, "lru_lambda_bwd"),
                      (C_FB, "fourier_b")):
            pv[:, c] = np.asarray(inp[nm][0], f32)[ch]
        pv[:, C_GL:C_GL + 4] = col8(inp["norm_lru_out"][0])
        pv[:, C_GFO:C_GFO + 4] = col8(inp["norm_fourier_out"][0])
        lruw = np.zeros((128, 4, 128), f32)
        for gi, nm in enumerate(("lru_wa_fwd", "lru_wx_fwd", "lru_wa_bwd", "lru_wx_bwd")):
            w = np.asarray(inp[nm][0], f32)
            for hh in range(2):
                lruw[hh * 64:(hh + 1) * 64, gi, hh * 64:(hh + 1) * 64] = w[2 * q + hh]
        fwbd = np.zeros((128, 128), f32)
        fw = np.asarray(inp["fourier_w"][0], f32)
        for hh in range(2):
            fwbd[hh * 64:(hh + 1) * 64, hh * 64:(hh + 1) * 64] = fw[2 * q + hh]
        ti_ = np.arange(4)[None, :, None]
        kc_ = np.arange(8)[None, None, :]
        p_ = np.arange(128)[:, None, None]
        u_ = ((ti_ // 2) * 2 + (kc_ // 4)) * 2 + (q // 2)
        idx = (u_ * 2048 + (kc_ % 4) * 512 + (q % 2) * 256 + (ti_ % 2) * 128 + p_).reshape(128, 32).astype(np.uint32)
        maps.append({"xT": xT, "metaT": metaT, "w1i": w1i, "w1o": w1o, "w2i": w2i, "w2o": w2o, "wsl": wsl,
                     "wop": wop, "pvec": pv, "lruw": lruw.reshape(128, 512), "fwbd": fwbd, "c64": c64,
                     "ident": ident, "c1s1": c1s1, "gtab": gtab, "idx": idx})
    return maps


_NC_CACHE = {}


def run(inp, debug=False, stop=None):
    if (debug, stop) not in _NC_CACHE:
        _NC_CACHE[(debug, stop)] = build_nc(debug, stop)
    nc = _NC_CACHE[(debug, stop)]
    maps = _prep_inputs(inp)
    res = run_bass_kernel_spmd(nc, maps, core_ids=list(range(8)))
    return res


def kernel(**inputs):
    res = run(inputs, DEBUG)
    out = np.empty((2, SEQ, D), np.float32)
    for core in range(8):
        b, q = core // 4, core % 4
        out[b, TOK * q:TOK * (q + 1), :] = np.asarray(res.results[core]["outT"]).T
    return out
```

```python
import math
from contextlib import ExitStack

import numpy as np
import concourse.bass as bass
import concourse.mybir as mybir
from concourse.bass_utils import run_bass_kernel_spmd

F32 = mybir.dt.float32
BF16 = mybir.dt.bfloat16
U32 = mybir.dt.uint32
ALU = mybir.AluOpType
AF = mybir.ActivationFunctionType

D = 1024
DFF = 2816
NF = 22
SEQ = 8192
NM = 16
T = SEQ + NM
TOK = 2048
NCOL = TOK + NM
N1, N2 = 108, 76
NV = 56
GROUPS = [[0, 1, 2, 3], [4, 5, 6, 7]]
EPS = 1e-6
DEBUG = False

C_G1, C_GM, C_G2, C_GF = 0, 8, 16, 24
C_CW, C_CB = 32, 36
C_BAF, C_BXF, C_LAMF, C_BAB, C_BXB, C_LAMB = 37, 38, 39, 40, 41, 42
C_FB = 45
C_GL, C_GFO = 46, 50

TILES5 = [(0, 512), (512, 512), (1024, 512), (1536, 512), (2048, 16)]
TILES4 = TILES5[:4]


class _Dummy:
    def then_inc(self, *a, **kw):
        return self


class _Proxy:
    def __init__(self, real, k):
        self._real, self._k = real, k

    def __getattr__(self, name):
        attr = getattr(self._real, name)
        if not callable(attr):
            return attr

        def f(*a, **kw):
            if self._k.stopped:
                return _Dummy()
            return attr(*a, **kw)
        return f


class K:
    def __init__(self, nc, es):
        self.nc, self.es = nc, es
        self.stopped = False
        self.eng = {"pe": nc.tensor, "act": nc.scalar, "dve": nc.vector, "pool": nc.gpsimd, "sp": nc.sync}
        self.eng = {n: _Proxy(e, self) for n, e in self.eng.items()}
        self.sem = {k: es.enter_context(nc.semaphore("sem_" + k)) for k in self.eng}
        self.cnt = {k: 0 for k in self.eng}
        self.seen = {}
        self.dsem, self.dcnt = {}, {}

    def sig(self, e, ins):
        ins.then_inc(self.sem[e], 1)
        self.cnt[e] += 1
        return ("e", e, self.cnt[e])

    def sw(self, e, ins):
        tok = self.sig(e, ins)
        self.eng[e].wait_ge(self.sem[e], tok[2])
        return tok

    def wait(self, e, *toks):
        for t in toks:
            if t is None:
                continue
            if isinstance(t, list):
                self.wait(e, *t)
                continue
            if t[0] == "e":
                _, p, v = t
                if p == e:
                    continue
                key = (e, p)
                if self.seen.get(key, 0) >= v:
                    continue
                self.eng[e].wait_ge(self.sem[p], v)
                self.seen[key] = v
            else:
                _, name, v = t
                key = (e, "d_" + name)
                if self.seen.get(key, 0) >= v:
                    continue
                self.eng[e].wait_ge(self.dsem[name], v)
                self.seen[key] = v

    def _dsem(self, name):
        if name not in self.dsem:
            self.dsem[name] = self.es.enter_context(self.nc.semaphore("d_" + name))
            self.dcnt[name] = 0
        return self.dsem[name]

    def dma(self, e, name, out, in_, **kw):
        s = self._dsem(name)
        self.eng[e].dma_start(out=out, in_=in_, **kw).then_inc(s, 16)
        self.dcnt[name] += 16
        return ("d", name, self.dcnt[name])

    def cc(self, name, ins, outs):
        s = self._dsem(name)
        self.eng["pool"].collective_compute("AllGather", ALU.bypass, replica_groups=GROUPS,
                                          ins=[ins], outs=[outs], dma_qos="P3").then_inc(s, 1)
        self.dcnt[name] += 1
        return ("d", name, self.dcnt[name])

    def gather(self, name, out, in_, idx_ap):
        s = self._dsem(name)
        self.eng["pool"].indirect_dma_start(out=out, out_offset=None, in_=in_,
                                          in_offset=bass.IndirectOffsetOnAxis(ap=idx_ap, axis=0)).then_inc(s, 16)
        self.dcnt[name] += 16
        return ("d", name, self.dcnt[name])


class _Stop(Exception):
    pass


def build_nc(debug=False, stop=None):
    nc = bass.Bass("TRN2", target_bir_lowering=False)
    dt = nc.dram_tensor
    xT = dt("xT", [D, TOK], F32, kind="ExternalInput").ap()
    metaT = dt("metaT", [D, NM], F32, kind="ExternalInput").ap()
    w1i = dt("w1i", [NF, 128, 2048], F32, kind="ExternalInput").ap()
    w1o = dt("w1o", [16, 128, 1408], F32, kind="ExternalInput").ap()
    w2i = dt("w2i", [NF, 128, 2048], F32, kind="ExternalInput").ap()
    w2o = dt("w2o", [16, 128, 1408], F32, kind="ExternalInput").ap()
    wsl = dt("wsl", [128, 3072], F32, kind="ExternalInput").ap()
    wop = dt("wop", [8, 128, 1024], F32, kind="ExternalInput").ap()
    pvec = dt("pvec", [128, NV], F32, kind="ExternalInput").ap()
    lruw = dt("lruw", [128, 512], F32, kind="ExternalInput").ap()
    fwbd = dt("fwbd", [128, 128], F32, kind="ExternalInput").ap()
    c64 = dt("c64", [128, 256], F32, kind="ExternalInput").ap()
    ident = dt("ident", [128, 128], F32, kind="ExternalInput").ap()
    c1s1 = dt("c1s1", [N1, 432], F32, kind="ExternalInput").ap()
    gtab = dt("gtab", [N2, N1 * 152], F32, kind="ExternalInput").ap()
    idx = dt("idx", [128, 32], U32, kind="ExternalInput").ap()
    outT = dt("outT", [D, TOK], F32, kind="ExternalOutput").ap()
    dbg = {}
    if debug:
        dbg["h1"] = dt("dbg_h1", [128, 8 * NCOL], F32, kind="ExternalOutput").ap()
        dbg["xc"] = dt("dbg_xc", [128, T], F32, kind="ExternalOutput").ap()
        dbg["gt"] = dt("dbg_gt", [128, T], F32, kind="ExternalOutput").ap()
        dbg["hf"] = dt("dbg_hf", [128, T], F32, kind="ExternalOutput").ap()
        dbg["yl"] = dt("dbg_yl", [128, T], F32, kind="ExternalOutput").ap()
        dbg["yf"] = dt("dbg_yf", [128, T], F32, kind="ExternalOutput").ap()

    HS = dt("hs", [128, 8 * NCOL], F32, kind="Internal").ap()
    HB = [dt(f"hb{j}", [1024, 512], BF16, kind="Internal").ap() for j in range(4)]
    HG = [dt(f"hg{j}", [4096, 512], BF16, kind="Internal").ap() for j in range(4)]
    YB = [dt(f"yb{i}", [512, 512], F32, kind="Internal").ap() for i in range(8)]
    GB = dt("gb", [16384, 512], F32, kind="Internal").ap()

    with ExitStack() as es:
        k = K(nc, es)
        pe, act, dve, pool, sp = (k.eng[n] for n in ("pe", "act", "dve", "pool", "sp"))

        def sb(stack, name, shape, dtype):
            return stack.enter_context(nc.sbuf_tensor(name, shape, dtype))

        PS = [es.enter_context(nc.psum_tensor(f"ps{i}", [128, 512], F32)) for i in range(8)]
        psf = [None] * 8
        PV = sb(es, "pv", [128, NV], F32)
        ONES = sb(es, "ones", [128, 128], BF16)
        IDENT = sb(es, "identsb", [128, 128], F32)
        EPSC = sb(es, "epsc", [128, 1], F32)
        RS = [sb(es, f"rs{i}", [128, 512], F32) for i in range(2)]
        SQ = sb(es, "sq", [128, 8, 512], BF16)
        HNM = sb(es, "hnm", [128, 8, NM], BF16)
        IDX = sb(es, "idxsb", [128, 32], U32)
        block = es.enter_context(nc.Block())

        st = {"rsf": [None, None], "sqf": None}
        t_const = []

        def emit_norm(src, dst, gcol, tiles, kgroups, src_ready, dst_free):
            toks = []
            for ti, (c0, n) in enumerate(tiles):
                for kcs in kgroups:
                    ri = emit_norm.cnt % 2
                    emit_norm.cnt += 1
                    k.wait("act", src_ready, st["sqf"])
                    for kc in kcs:
                        a = act.activation(out=SQ[:, kc, 0:n], in_=src[:, kc, c0:c0 + n], func=AF.Square)
                    t_sq = k.sig("act", a)
                    k.wait("pe", t_sq, t_const, psf[6])
                    for i, kc in enumerate(kcs):
                        mm = pe.matmul(PS[6][:, 0:n], lhsT=ONES[:], rhs=SQ[:, kc, 0:n],
                                       start=(i == 0), stop=(i == len(kcs) - 1))
                    t_mm = k.sig("pe", mm)
                    st["sqf"] = t_mm
                    k.wait("act", t_mm, st["rsf"][ri])
                    a = act.activation(out=RS[ri][:, 0:n], in_=PS[6][:, 0:n], func=AF.Sqrt,
                                       bias=EPSC[:, 0:1], scale=1.0 / (128 * len(kcs)))
                    t_a = k.sig("act", a)
                    psf[6] = t_a
                    k.wait("dve", t_a, dst_free)
                    k.sw("dve", dve.reciprocal(out=RS[ri][:, 0:n], in_=RS[ri][:, 0:n]))
                    for kc in kcs:
                        d = dve.scalar_tensor_tensor(out=dst[:, kc, c0:c0 + n], in0=src[:, kc, c0:c0 + n],
                                                     scalar=PV[:, gcol + kc:gcol + kc + 1], in1=RS[ri][:, 0:n],
                                                     op0=ALU.mult, op1=ALU.mult)
                    t_d = k.sig("dve", d)
                    st["rsf"][ri] = t_d
                toks.append(t_d)
            return toks
        emit_norm.cnt = 0

        GW = 1040
        NSLOT = 5

        def alloc_ffn(stack, tag):
            fb = {"G": sb(stack, "g" + tag, [128, 11, GW], BF16),
                  "WI": [sb(stack, f"wi{tag}{i}", [128, 8, 256], BF16) for i in range(NSLOT)],
                  "WO": [sb(stack, f"wo{tag}{i}", [128, 11, 128], BF16) for i in range(NSLOT)],
                  "SS": [sb(stack, f"ss{tag}{i}", [128, 512], F32) for i in range(2)],
                  "wif": [None] * NSLOT, "wof": [None] * NSLOT, "ssf": [None, None],
                  "cnt": {"i": 0, "o": 0, "g": 0, "p": 0}, "g_free": None, "pre": {}}
            return fb

        FFN_JOBS = []
        for half in range(2):
            FFN_JOBS += [("i", half * 11 + jj, jj) for jj in range(11)]
            FFN_JOBS += [("o", half * 8 + i, i) for i in range(8)]

        def ffn_issue(fb, loaded, n, wi_d, wo_d):
            if n >= len(FFN_JOBS) or n in loaded:
                return
            kind, a, _ = FFN_JOBS[n]
            c = fb["cnt"][kind]
            fb["cnt"][kind] += 1
            slot = c % NSLOT
            if kind == "i":
                k.wait("pool", fb["wif"][slot])
                loaded[n] = (slot, k.dma("pool", f"wi{slot}", fb["WI"][slot][:].rearrange("p a b -> p (a b)"), wi_d[a],
                                          max_dma_last_dim=4096))
            else:
                k.wait("pool", fb["wof"][slot])
                loaded[n] = (slot, k.dma("pool", f"wo{slot}", fb["WO"][slot][:].rearrange("p a b -> p (a b)"), wo_d[a],
                                          max_dma_last_dim=4096))

        def ffn_prefetch(fb, wi_d, wo_d):
            for n in range(NSLOT - 1):
                ffn_issue(fb, fb["pre"], n, wi_d, wo_d)

        def emit_ffn(fb, H, HN, wi_d, wo_d, tiles, hn_ready):
            G, WI, WO, SS = fb["G"], fb["WI"], fb["WO"], fb["SS"]
            wif, wof, ssf, cnt = fb["wif"], fb["wof"], fb["ssf"], fb["cnt"]
            loaded = fb["pre"]
            fb["pre"] = {}
            base = tiles[0][0]
            for n in range(NSLOT - 1):
                ffn_issue(fb, loaded, n, wi_d, wo_d)
            g_ready = None
            g_free = fb["g_free"]
            h_tok = None
            for n, (kind, a, b) in enumerate(FFN_JOBS):
                ffn_issue(fb, loaded, n + NSLOT - 1, wi_d, wo_d)
                slot, t_w = loaded[n]
                if kind == "i":
                    jj = b
                    for (c0, nn) in tiles:
                        pb = cnt["g"] % 2
                        cnt["g"] += 1
                        PG, PU = PS[pb], PS[2 + pb]
                        k.wait("pe", t_w, hn_ready, psf[pb], psf[2 + pb])
                        for kc in range(8):
                            pe.matmul(PG[:, 0:nn], lhsT=WI[slot][:, kc, 0:128], rhs=HN[:, kc, c0:c0 + nn],
                                      start=(kc == 0), stop=(kc == 7))
                        for kc in range(8):
                            mm = pe.matmul(PU[:, 0:nn], lhsT=WI[slot][:, kc, 128:256], rhs=HN[:, kc, c0:c0 + nn],
                                           start=(kc == 0), stop=(kc == 7))
                        t_mm = k.sig("pe", mm)
                        k.wait("act", t_mm, ssf[pb])
                        ai = act.activation(out=SS[pb][:, 0:nn], in_=PG[:, 0:nn], func=AF.Silu)
                        t_a = k.sig("act", ai)
                        k.wait("dve", t_a, g_free)
                        d = dve.tensor_tensor(out=G[:, jj, c0 - base:c0 - base + nn], in0=SS[pb][:, 0:nn],
                                              in1=PU[:, 0:nn], op=ALU.mult)
                        t_d = k.sig("dve", d)
                        psf[pb] = t_d
                        psf[2 + pb] = t_d
                        ssf[pb] = t_d
                    wif[slot] = t_mm
                    g_ready = t_d
                else:
                    i = b
                    for (c0, nn) in tiles:
                        pb = 4 + cnt["p"] % 2
                        cnt["p"] += 1
                        k.wait("pe", t_w, g_ready, psf[pb])
                        for fc in range(11):
                            mm = pe.matmul(PS[pb][:, 0:nn], lhsT=WO[slot][:, fc, :],
                                           rhs=G[:, fc, c0 - base:c0 - base + nn],
                                           start=(fc == 0), stop=(fc == 10))
                        t_mm = k.sig("pe", mm)
                        k.wait("dve", t_mm)
                        d = dve.scalar_tensor_tensor(out=H[:, i, c0:c0 + nn], in0=PS[pb][:, 0:nn], scalar=0.5,
                                                     in1=H[:, i, c0:c0 + nn], op0=ALU.mult, op1=ALU.add)
                        t_d = k.sig("dve", d)
                        psf[pb] = t_d
                    wof[slot] = t_mm
                    g_free = t_mm
                    h_tok = t_d
            fb["g_free"] = g_free
            return h_tok

        def emit_norm_rel(src, dst, c0, n, src_ready):
            t_d = None
            for gi, (kcs, gcol) in enumerate([(list(range(4)), C_GL), (list(range(4, 8)), C_GFO - 4)]):
                ri = emit_norm.cnt % 2
                emit_norm.cnt += 1
                k.wait("act", src_ready, st["sqf"])
                for kc in kcs:
                    a = act.activation(out=SQ[:, kc, 0:n], in_=src[:, kc, 0:n], func=AF.Square)
                t_sq = k.sig("act", a)
                k.wait("pe", t_sq, psf[6])
                for i, kc in enumerate(kcs):
                    mm = pe.matmul(PS[6][:, 0:n], lhsT=ONES[:], rhs=SQ[:, kc, 0:n], start=(i == 0), stop=(i == 3))
                t_mm = k.sig("pe", mm)
                st["sqf"] = t_mm
                k.wait("act", t_mm, st["rsf"][ri])
                a = act.activation(out=RS[ri][:, 0:n], in_=PS[6][:, 0:n], func=AF.Sqrt, bias=EPSC[:, 0:1],
                                   scale=1.0 / 512.0)
                t_a = k.sig("act", a)
                psf[6] = t_a
                k.wait("dve", t_a)
                k.sw("dve", dve.reciprocal(out=RS[ri][:, 0:n], in_=RS[ri][:, 0:n]))
                for kc in kcs:
                    d = dve.scalar_tensor_tensor(out=dst[:, kc, c0:c0 + n], in0=src[:, kc, 0:n],
                                                 scalar=PV[:, gcol + kc:gcol + kc + 1], in1=RS[ri][:, 0:n],
                                                 op0=ALU.mult, op1=ALU.mult)
                t_d = k.sig("dve", d)
                st["rsf"][ri] = t_d
            return t_d

        def stop_here(name, toks):
            if stop == name:
                for e in ("sp", "pool", "dve", "act", "pe"):
                    k.wait(e, toks)
                k.stopped = True

        def _body(g):
            global_toks = []
            t_pv = k.dma("sp", "pv", PV[:], pvec)
            t_id = k.dma("sp", "ident", IDENT[:], ident)
            t_ix = k.dma("sp", "idx", IDX[:], idx)
            dve.memset(ONES[:], 1.0)
            m = dve.memset(EPSC[:], EPS)
            t_const.extend([k.sig("dve", m), t_pv, t_id])

            with ExitStack() as sa:
                HN = sb(sa, "hn_a", [128, 8, NCOL], BF16)
                HN2 = sb(sa, "hn2_a", [128, 8, 1024], BF16)
                H = sb(sa, "h_a", [128, 8, NCOL], F32)
                fb = alloc_ffn(sa, "a")
                ffn_prefetch(fb, w1i, w1o)
                ld = []
                for kc in range(8):
                    ld.append(k.dma("sp", "xin", H[:, kc, 0:TOK], xT[kc * 128:(kc + 1) * 128, :]))
                    ld.append(k.dma("sp", "min", H[:, kc, TOK:NCOL], metaT[kc * 128:(kc + 1) * 128, :]))
                k.wait("dve", t_pv)
                tn = emit_norm(H, HN, C_G1, TILES5, [list(range(8))], [ld[-1], ld[-2], t_pv], None)
                TA, TB = TILES5[:2], TILES5[2:]
                t_hA = emit_ffn(fb, H, HN, w1i, w1o, TA, tn[1])
                ffn_prefetch(fb, w1i, w1o)
                t_ag, t_bs = [], []
                tnA = emit_norm(H, HN2, C_GM, TA, [list(range(8))], t_hA, None)
                for j in range(2):
                    k.wait("sp", tnA[j])
                    t_b = k.dma("sp", f"hb{j}", HB[j].rearrange("(kc p) c -> p kc c", p=128),
                                HN2[:, :, j * 512:(j + 1) * 512])
                    t_bs.append(t_b)
                    k.wait("pool", t_b)
                    t_ag.append(k.cc(f"agf{j}", HB[j], HG[j]))
                t_h = emit_ffn(fb, H, HN, w1i, w1o, TB, tn[-1])
                if debug:
                    k.wait("sp", t_h)
                    global_toks.append(k.dma("sp", "dbg", dbg["h1"], H[:].rearrange("p a b -> p (a b)")))
                stop_here("ffn1", [t_h] + global_toks + t_bs + t_ag)
                tnB = emit_norm(H, HN, C_GM, TB, [list(range(8))], t_h, t_h)
                k.wait("sp", t_h)
                t_spill = k.dma("sp", "hs", HS, H[:].rearrange("p a b -> p (a b)"))
                k.wait("pool", tnB[2])
                cp = pool.tensor_copy(out=HNM[:], in_=HN[:, :, TOK:NCOL])
                t_hnm = k.sig("pool", cp)
                for j in range(2, 4):
                    k.wait("sp", tnB[j - 2])
                    t_b = k.dma("sp", f"hb{j}", HB[j].rearrange("(kc p) c -> p kc c", p=128),
                                HN[:, :, j * 512:(j + 1) * 512])
                    t_bs.append(t_b)
                    k.wait("pool", t_b)
                    t_ag.append(k.cc(f"agf{j}", HB[j], HG[j]))
                t_a_done = [t_spill, t_hnm] + t_bs + global_toks
                stop_here("agf", t_a_done + t_ag)
                for e in ("sp", "pool", "dve", "act", "pe"):
                    k.wait(e, t_a_done)

            with ExitStack() as sbk:
                V = sb(sbk, "v", [128, T], BF16)
                with ExitStack() as sl:
                    GT = sb(sl, "gt", [128, T], F32)
                    XC = sb(sl, "xc", [128, T], F32)
                    XCB = sb(sl, "xcb", [128, T], BF16)
                    sx = ExitStack()
                    XL = sb(sx, "xl", [128, T + 3], F32)
                    m1 = pool.memset(XL[:, 0:2], 0.0)
                    m1 = pool.memset(XL[:, T + 2:T + 3], 0.0)
                    t_pad = k.sig("pool", m1)
                    with ExitStack() as s1:
                        WSL = sb(s1, "wslsb", [128, 8, 384], BF16)
                        HNA = [sb(s1, f"hna{i}", [128, 8, 512], BF16) for i in range(4)]
                        t_wsl = k.dma("pool", "wsl", WSL[:].rearrange("p a b -> p (a b)"), wsl, max_dma_last_dim=4096)
                        hnaf = [None] * 4
                        cntb = 0
                        nload = 0
                        t_last_act = None

                        def inproj(rhs_of_kc, n, pos, t_in):
                            nonlocal cntb, t_last_act
                            t_mm = None
                            for c in range(3):
                                pb = 4 + cntb % 2
                                cntb += 1
                                k.wait("pe", t_in, t_wsl, psf[pb])
                                for kc in range(8):
                                    mm = pe.matmul(PS[pb][:, 0:n], lhsT=WSL[:, kc, c * 128:(c + 1) * 128],
                                                   rhs=rhs_of_kc(kc), start=(kc == 0), stop=(kc == 7))
                                t_mm = k.sig("pe", mm)
                                k.wait("act", t_mm, t_pad)
                                dest = [XL[:, 2 + pos:2 + pos + n], GT[:, pos:pos + n], V[:, pos:pos + n]][c]
                                a = act.activation(out=dest, in_=PS[pb][:, 0:n], func=AF.Copy)
                                t_last_act = k.sig("act", a)
                                psf[pb] = t_last_act
                            return t_mm

                        inproj(lambda kc: HNM[:, kc, :], NM, 0, t_hnm)
                        for j in range(4):
                            for r in range(4):
                                slot = nload % 4
                                nload += 1
                                k.wait("sp", t_ag[j], hnaf[slot])
                                t_l = k.dma("sp", f"hna{slot}", HNA[slot][:],
                                            HG[j][r * 1024:(r + 1) * 1024, :].rearrange("(kc p) c -> p kc c", p=128))
                                hnaf[slot] = inproj(lambda kc, s=slot: HNA[s][:, kc, :], 512,
                                                    NM + TOK * r + 512 * j, t_l)
                        t_b1 = t_last_act
                        if debug and stop == "b1":
                            k.wait("sp", t_b1)
                            global_toks.append(k.dma("sp", "dbg", dbg["xc"], XL[:, 2:T + 2]))
                            global_toks.append(k.dma("sp", "dbg", dbg["gt"], GT[:, 0:T]))
                        stop_here("b1", [t_b1] + global_toks)
                        k.wait("pe", t_b1)
                        k.wait("pool", t_b1)
                        k.wait("dve", t_b1)
                        k.wait("sp", t_b1)

                    chunks_c = [(c, 2052) for c in range(0, T, 2052)]
                    t_xc = []
                    for (c0, n) in chunks_c:
                        k.wait("pool", t_pv)
                        p0 = pool.tensor_scalar(out=XC[:, c0:c0 + n], in0=XL[:, c0 + 2:c0 + 2 + n],
                                                scalar1=PV[:, C_CW + 2:C_CW + 3], scalar2=PV[:, C_CB:C_CB + 1],
                                                op0=ALU.mult, op1=ALU.add)
                        t_p0 = k.sig("pool", p0)
                        k.wait("dve", t_p0)
                        for kk in (0, 1, 3):
                            d = dve.scalar_tensor_tensor(out=XC[:, c0:c0 + n], in0=XL[:, c0 + kk:c0 + kk + n],
                                                         scalar=PV[:, C_CW + kk:C_CW + kk + 1], in1=XC[:, c0:c0 + n],
                                                         op0=ALU.mult, op1=ALU.add)
                            t_d = k.sw("dve", d)
                        k.wait("pool", t_d)
                        p1 = pool.tensor_copy(out=XCB[:, c0:c0 + n], in_=XC[:, c0:c0 + n])
                        t_xc = [t_d, k.sig("pool", p1)]
                    if debug:
                        k.wait("sp", t_xc)
                        global_toks.append(k.dma("sp", "dbg", dbg["xc"], XC[:, 0:T]))
                        global_toks.append(k.dma("sp", "dbg", dbg["gt"], GT[:, 0:T]))
                    stop_here("conv", t_xc + global_toks)
                    for e in ("sp", "pool", "dve", "act", "pe"):
                        k.wait(e, t_xc, global_toks)
                    sx.close()

                    with ExitStack() as s2:
                        HF = sb(s2, "hf", [128, T], F32)
                        LW = sb(s2, "lw", [128, 4, 128], BF16)
                        CL = sb(s2, "cl", [128, 8], F32)
                        RB = [sb(s2, f"rb{i}", [128, 1024], F32) for i in range(2)]
                        IB = [sb(s2, f"ib{i}", [128, 1024], F32) for i in range(2)]
                        AB = [sb(s2, f"ab{i}", [128, 1024], F32) for i in range(2)]
                        TB = [sb(s2, f"tb{i}", [128, 1024], F32) for i in range(2)]
                        t_lw = k.dma("pool", "lw", LW[:].rearrange("p a b -> p (a b)"), lruw)
                        k.wait("act", t_pv)
                        act.activation(out=CL[:, 4:5], in_=PV[:, C_LAMF:C_LAMF + 1], func=AF.Exp, scale=-1.0)
                        a = act.activation(out=CL[:, 5:6], in_=PV[:, C_LAMB:C_LAMB + 1], func=AF.Exp, scale=-1.0)
                        t_e = k.sig("act", a)
                        k.wait("dve", t_e)
                        E = CL[:, 4:6]
                        Pp = CL[:, 6:8]
                        k.sw("dve", dve.tensor_scalar(out=Pp, in0=E, scalar1=1.0 / 7.0, scalar2=-1.0 / 6.0,
                                                      op0=ALU.mult, op1=ALU.add))
                        for cst in (1.0 / 5.0, -1.0 / 4.0, 1.0 / 3.0, -0.5, 1.0):
                            k.sw("dve", dve.tensor_tensor(out=Pp, in0=Pp, in1=E, op=ALU.mult))
                            k.sw("dve", dve.tensor_scalar(out=Pp, in0=Pp, scalar1=cst, scalar2=None, op0=ALU.add))
                        k.sw("dve", dve.tensor_tensor(out=Pp, in0=Pp, in1=E, op=ALU.mult))
                        dve.tensor_scalar(out=CL[:, 0:2], in0=Pp, scalar1=-8.0, scalar2=None, op0=ALU.mult)
                        d = dve.tensor_scalar(out=CL[:, 2:4], in0=Pp, scalar1=-16.0, scalar2=None, op0=ALU.mult)
                        t_cl = k.sw("dve", d)
                        stop_here("setup", [t_cl, t_lw] + global_toks)

                        chunks = [(c, 1024) for c in range(0, SEQ, 1024)] + [(SEQ, NM)]
                        setf = [None, None]
                        GELU_A = math.sqrt(0.044715)

                        def fence(e, ins, cond):
                            return k.sw(e, ins) if cond else k.sig(e, ins)

                        def lru_pass(direction):
                            gi = 0 if direction == 0 else 2
                            cba, cbx = (C_BAF, C_BXF) if direction == 0 else (C_BAB, C_BXB)
                            order = chunks if direction == 0 else chunks[::-1]
                            first = True
                            t_out = None
                            for ci, (c0, n) in enumerate(order):
                                s_ = ci % 2
                                small = n < 128
                                R, I, A, Tt = RB[s_], IB[s_], AB[s_], TB[s_]
                                subs = [(s0, min(512, n - s0)) for s0 in range(0, n, 512)]
                                t_mms = []
                                for si, (s0, ns) in enumerate(subs):
                                    b0, b1 = PS[4 * s_ + si], PS[4 * s_ + 2 + si]
                                    k.wait("pe", t_lw, t_xc, psf[4 * s_ + si], psf[4 * s_ + 2 + si])
                                    pe.matmul(b0[:, 0:ns], lhsT=LW[:, gi, :], rhs=XCB[:, c0 + s0:c0 + s0 + ns],
                                              start=True, stop=True)
                                    mm = pe.matmul(b1[:, 0:ns], lhsT=LW[:, gi + 1, :],
                                                   rhs=XCB[:, c0 + s0:c0 + s0 + ns], start=True, stop=True)
                                    t_mms.append(k.sig("pe", mm))
                                k.wait("act", t_mms, setf[s_], t_cl)
                                for si, (s0, ns) in enumerate(subs):
                                    act.activation(out=R[:, s0:s0 + ns], in_=PS[4 * s_ + si][:, 0:ns], func=AF.Sigmoid,
                                                   bias=PV[:, cba:cba + 1], scale=1.0)
                                    a = act.activation(out=I[:, s0:s0 + ns], in_=PS[4 * s_ + 2 + si][:, 0:ns],
                                                       func=AF.Sigmoid, bias=PV[:, cbx:cbx + 1], scale=1.0)
                                    t_a = fence("act", a, small)
                                    psf[4 * s_ + si] = t_a
                                    psf[4 * s_ + 2 + si] = t_a
                                t_gs = None
                                if direction == 0:
                                    gt = GT[:, c0:c0 + n]
                                    a = act.activation(out=Tt[:, 0:n], in_=gt, func=AF.Square, scale=GELU_A)
                                    t_q = k.sig("act", a)
                                    k.wait("dve", t_q)
                                    d = dve.scalar_tensor_tensor(out=Tt[:, 0:n], in0=Tt[:, 0:n], scalar=1.0, in1=gt,
                                                                 op0=ALU.add, op1=ALU.mult)
                                    t_u = k.sig("dve", d)
                                    k.wait("act", t_u)
                                    a = act.activation(out=Tt[:, 0:n], in_=Tt[:, 0:n], func=AF.Sigmoid,
                                                       scale=1.5957691216057308)
                                    t_s = k.sig("act", a)
                                    k.wait("pool", t_s)
                                    t_gs = k.sig("pool", pool.tensor_tensor(out=gt, in0=gt, in1=Tt[:, 0:n], op=ALU.mult))
                                act.activation(out=A[:, 0:n], in_=R[:, 0:n], func=AF.Exp,
                                               scale=CL[:, direction:direction + 1])
                                k.wait("act", t_gs)
                                a = act.activation(out=Tt[:, 0:n], in_=R[:, 0:n], func=AF.Exp,
                                                   scale=CL[:, 2 + direction:3 + direction])
                                fence("act", a, small)
                                a = act.activation(out=Tt[:, 0:n], in_=Tt[:, 0:n], func=AF.Sqrt, bias=1.0, scale=-1.0)
                                t_act = k.sig("act", a)
                                k.wait("dve", t_act)
                                fence("dve", dve.tensor_tensor(out=I[:, 0:n], in0=I[:, 0:n], in1=Tt[:, 0:n], op=ALU.mult),
                                      small)
                                d = dve.tensor_tensor(out=I[:, 0:n], in0=I[:, 0:n], in1=XC[:, c0:c0 + n], op=ALU.mult)
                                fence("dve", d, small or direction == 1)
                                if direction == 0:
                                    init = 0.0 if first else HF[:, c0 - 1:c0]
                                    d = dve.tensor_tensor_scan(out=HF[:, c0:c0 + n], data0=A[:, 0:n], data1=I[:, 0:n],
                                                               initial=init, op0=ALU.mult, op1=ALU.add)
                                    t_out = k.sw("dve", d)
                                else:
                                    init = 0.0 if first else CL[:, 4:5]
                                    k.sw("dve", dve.tensor_tensor_scan(out=R[:, 0:n][:, ::-1], data0=A[:, 0:n][:, ::-1],
                                                                       data1=I[:, 0:n][:, ::-1], initial=init,
                                                                       op0=ALU.mult, op1=ALU.add))
                                    k.sw("dve", dve.tensor_copy(out=CL[:, 4:5], in_=R[:, 0:1]))
                                    fence("dve", dve.tensor_tensor(out=R[:, 0:n], in0=R[:, 0:n], in1=HF[:, c0:c0 + n],
                                                                   op=ALU.add), small)
                                    d = dve.tensor_tensor(out=GT[:, c0:c0 + n], in0=R[:, 0:n], in1=GT[:, c0:c0 + n],
                                                          op=ALU.mult)
                                    t_out = k.sig("dve", d)
                                setf[s_] = t_out
                                first = False
                            return t_out

                        t_f = lru_pass(0)
                        if debug:
                            k.wait("sp", t_f)
                            global_toks.append(k.dma("sp", "dbg", dbg["hf"], HF[:, 0:T]))
                        stop_here("lruf", [t_f] + global_toks)
                        t_yl = lru_pass(1)
                        if debug:
                            k.wait("sp", t_yl)
                            global_toks.append(k.dma("sp", "dbg", dbg["yl"], GT[:, 0:T]))
                        stop_here("lrub", [t_yl] + global_toks)
                        t_agl = {}
                        lru_bounce = {}
                        for hh in range(2):
                            for dp in range(2):
                                u = (hh * 2 + 0) * 2 + dp
                                k.wait("sp", t_yl)
                                for e_ in range(2):
                                    a0 = NM + TOK * (2 * dp + e_) + 1024 * hh
                                    t_b = k.dma("sp", f"yb{u}",
                                                YB[u][e_ * 256:(e_ + 1) * 256, :].rearrange("(t p) c -> p t c", p=128),
                                                GT[:, a0:a0 + 1024].rearrange("p (t c) -> p t c", c=512))
                                lru_bounce[u] = t_b
                                if hh == 0:
                                    k.wait("pool", t_b)
                                    t_agl[u] = k.cc(f"agb{u}", YB[u], GB[u * 2048:(u + 1) * 2048, :])
                        t_l_done = [t_yl] + global_toks + list(lru_bounce.values())
                        for e in ("sp", "pool", "dve", "act", "pe"):
                            k.wait(e, t_l_done)

                with ExitStack() as s3:
                    Z = sb(s3, "z", [N1, N2, 256], BF16)
                    AS = sb(s3, "as", [N2, 128, 216], BF16)
                    GTB = sb(s3, "gtb", [N2, N1, 152], BF16)
                    YF = sb(s3, "yf", [128, N2, N1], F32)
                    C1 = sb(s3, "c1", [N1, 2, 216], BF16)
                    C64 = sb(s3, "c64sb", [128, 2, 128], F32)
                    FW = sb(s3, "fwsb", [128, 128], F32)
                    MB = sb(s3, "mb", [128, 256], BF16)
                    t_c64 = k.dma("sp", "c64", C64[:].rearrange("p a b -> p (a b)"), c64)
                    t_fw = k.dma("sp", "fw", FW[:], fwbd)
                    t_c1 = k.dma("pool", "c1", C1[:].rearrange("p a b -> p (a b)"), c1s1)
                    t_gt = None
                    for q4 in range(4):
                        t_gt = k.dma("pool", "gtb", GTB[:, q4 * 27:(q4 + 1) * 27, :],
                                     gtab.rearrange("p (a b) -> p a b", b=152)[:, q4 * 27:(q4 + 1) * 27, :])
                    CH = sb(s3, "c64h", [128, 2, 128], BF16)
                    CLo = sb(s3, "c64l", [128, 2, 128], BF16)
                    WH = sb(s3, "fwh", [128, 128], BF16)
                    WLo = sb(s3, "fwl", [128, 128], BF16)
                    k.wait("dve", t_c64, t_fw)
                    k.sw("dve", dve.tensor_copy(out=CH[:], in_=C64[:]))
                    k.sw("dve", dve.tensor_copy(out=WH[:], in_=FW[:]))
                    k.sw("dve", dve.tensor_tensor(out=CLo[:], in0=C64[:], in1=CH[:], op=ALU.subtract))
                    t_hl = k.sw("dve", dve.tensor_tensor(out=WLo[:], in0=FW[:], in1=WH[:], op=ALU.subtract))
                    k.wait("pe", t_hl, psf[0])
                    for u in range(2):
                        pe.matmul(PS[0][:, u * 128:(u + 1) * 128], lhsT=CH[:, u, :], rhs=WH[:], start=True, stop=False)
                        pe.matmul(PS[0][:, u * 128:(u + 1) * 128], lhsT=CH[:, u, :], rhs=WLo[:], start=False, stop=False)
                        mm = pe.matmul(PS[0][:, u * 128:(u + 1) * 128], lhsT=CLo[:, u, :], rhs=WH[:], start=False, stop=True)
                    t_mm = k.sig("pe", mm)
                    k.wait("act", t_mm)
                    a = act.activation(out=MB[:], in_=PS[0][:, 0:256], func=AF.Copy)
                    t_mb = k.sig("act", a)
                    psf[0] = t_mb
                    cn = 0
                    t_z = None
                    for t2p in range(0, N2, 2):
                        pb = cn % 4
                        cn += 1
                        k.wait("pe", t_mb, psf[pb])
                        for u in range(2):
                            t2 = t2p + u
                            mm = pe.matmul(PS[pb][0:N1, u * 256:(u + 1) * 256], lhsT=V[:, t2:T:N2], rhs=MB[:],
                                           start=True, stop=True)
                        t_mm = k.sig("pe", mm)
                        eng = "act" if (cn % 2) else "dve"
                        k.wait(eng, t_mm)
                        dst = Z[:, t2p:t2p + 2, :].rearrange("p a b -> p (a b)")
                        if eng == "act":
                            i_ = act.activation(out=dst, in_=PS[pb][0:N1, 0:512], func=AF.Copy)
                        else:
                            i_ = dve.tensor_copy(out=dst, in_=PS[pb][0:N1, 0:512])
                        t_z0 = k.sig(eng, i_)
                        psf[pb] = t_z0
                        t_z = [t_z0] if t_z is None else (t_z + [t_z0])[-2:]
                    t_as = None
                    for jp in range(0, 128, 2):
                        pb = cn % 4
                        cn += 1
                        k.wait("pe", t_z, t_c1, psf[pb])
                        for u in range(2):
                            j = jp + u
                            pe.matmul(PS[pb][0:N2, u * 216:(u + 1) * 216], lhsT=Z[:, :, j], rhs=C1[:, 0, :],
                                      start=True, stop=False)
                            mm = pe.matmul(PS[pb][0:N2, u * 216:(u + 1) * 216], lhsT=Z[:, :, 128 + j], rhs=C1[:, 1, :],
                                           start=False, stop=True)
                        t_mm = k.sig("pe", mm)
                        eng = "act" if (cn % 2) else "dve"
                        k.wait(eng, t_mm)
                        dst = AS[:, jp:jp + 2, :].rearrange("p a b -> p (a b)")
                        if eng == "act":
                            i_ = act.activation(out=dst, in_=PS[pb][0:N2, 0:432], func=AF.Copy)
                        else:
                            i_ = dve.tensor_copy(out=dst, in_=PS[pb][0:N2, 0:432])
                        t_a0 = k.sig(eng, i_)
                        psf[pb] = t_a0
                        t_as = [t_a0] if t_as is None else (t_as + [t_a0])[-2:]
                    t_yf = None
                    for kg in range(0, N1, 6):
                        pb = cn % 4
                        cn += 1
                        k.wait("pe", t_as, t_gt, psf[pb])
                        for u in range(6):
                            k1 = kg + u
                            pe.matmul(PS[pb][:, u * N2:(u + 1) * N2], lhsT=AS[:, :, k1], rhs=GTB[:, k1, 0:N2],
                                      start=True, stop=False)
                            mm = pe.matmul(PS[pb][:, u * N2:(u + 1) * N2], lhsT=AS[:, :, N1 + k1],
                                           rhs=GTB[:, k1, N2:2 * N2], start=False, stop=True)
                        t_mm = k.sig("pe", mm)
                        k.wait("dve", t_mm, t_pv)
                        i_ = dve.tensor_scalar(out=YF[:, :, kg:kg + 6],
                                               in0=PS[pb][:, 0:6 * N2].rearrange("p (a b) -> p b a", b=N2),
                                               scalar1=PV[:, C_FB:C_FB + 1], scalar2=None, op0=ALU.add)
                        t_y0 = k.sig("dve", i_)
                        psf[pb] = t_y0
                        t_yf = t_y0
                    YFf = YF[:].rearrange("p a b -> p (a b)")
                    dbg_t = []
                    if debug:
                        k.wait("sp", t_yf)
                        dbg_t.append(k.dma("sp", "dbg", dbg["yf"], YFf[:, 0:T]))
                    stop_here("dft", [t_yf] + dbg_t + global_toks)
                    t_agf = {}
                    four_bounce = {}
                    for hh in range(2):
                        for dp in range(2):
                            u = (hh * 2 + 1) * 2 + dp
                            k.wait("sp", t_yf)
                            for e_ in range(2):
                                a0 = NM + TOK * (2 * dp + e_) + 1024 * hh
                                t_b = k.dma("sp", f"yb{u}",
                                            YB[u][e_ * 256:(e_ + 1) * 256, :].rearrange("(t p) c -> p t c", p=128),
                                            YFf[:, a0:a0 + 1024].rearrange("p (t c) -> p t c", c=512))
                            four_bounce[u] = t_b
                    for hh in range(2):
                        for dp in range(2):
                            u = (hh * 2 + 1) * 2 + dp
                            k.wait("pool", four_bounce[u])
                            t_agf[u] = k.cc(f"agb{u}", YB[u], GB[u * 2048:(u + 1) * 2048, :])
                        if hh == 0:
                            for dp in range(2):
                                u = (1 * 2 + 0) * 2 + dp
                                k.wait("pool", lru_bounce[u])
                                t_agl[u] = k.cc(f"agb{u}", YB[u], GB[u * 2048:(u + 1) * 2048, :])
                    t_f_done = [t_yf] + dbg_t + list(four_bounce.values())
                    for e in ("sp", "pool", "dve", "act", "pe"):
                        k.wait(e, t_f_done)

            t_all_back = list(t_agl.values()) + list(t_agf.values())
            stop_here("back", t_all_back + global_toks)
            with ExitStack() as sc:
                HN = sb(sc, "hn_c", [128, 8, NCOL], BF16)
                H = sb(sc, "h_c", [128, 8, NCOL], F32)
                t_hl = k.dma("sp", "hld", H[:].rearrange("p a b -> p (a b)"), HS)
                YT = [sb(sc, f"yt{i}", [128, 8, 512], F32) for i in range(1)]
                WP = [sb(sc, f"wp{i}", [128, 8, 128], BF16) for i in range(3)]
                fb = alloc_ffn(sc, "c")
                ytf = [None, None]
                wpf = [None] * 3
                cnp = 0
                nwp = 0
                outs = []
                for hh in range(2):
                    tiles = TILES4[2 * hh:2 * hh + 2]
                    t_units = [t_agl[(hh * 2 + 0) * 2 + dp] for dp in range(2)] + \
                              [t_agf[(hh * 2 + 1) * 2 + dp] for dp in range(2)]
                    tns = {}
                    for ti in (2 * hh, 2 * hh + 1):
                        c0, n = TILES4[ti]
                        yb_ = 0
                        k.wait("pool", t_units, ytf[yb_], t_ix)
                        tg = None
                        for kc in range(8):
                            tg = k.gather(f"yt{yb_}", YT[yb_][:, kc, :], GB, IDX[:, ti * 8 + kc:ti * 8 + kc + 1])
                        toks = emit_norm_rel(YT[yb_], HN, c0, n, tg)
                        ytf[yb_] = toks
                        tns[ti] = toks
                    ffn_prefetch(fb, w2i, w2o)
                    t_h2 = None
                    t_mm = None
                    for i in range(8):
                        slot = nwp % 3
                        nwp += 1
                        k.wait("pool", wpf[slot])
                        t_w = k.dma("pool", f"wp{slot}", WP[slot][:].rearrange("p a b -> p (a b)"), wop[i])
                        for ti in (2 * hh, 2 * hh + 1):
                            c0, n = TILES4[ti]
                            pb = 4 + cnp % 2
                            cnp += 1
                            k.wait("pe", t_w, tns[ti], psf[pb])
                            for kc in range(8):
                                mm = pe.matmul(PS[pb][:, 0:n], lhsT=WP[slot][:, kc, :], rhs=HN[:, kc, c0:c0 + n],
                                               start=(kc == 0), stop=(kc == 7))
                            t_mm = k.sig("pe", mm)
                            k.wait("dve", t_mm, t_hl)
                            d = dve.tensor_tensor(out=H[:, i, c0:c0 + n], in0=PS[pb][:, 0:n], in1=H[:, i, c0:c0 + n],
                                                  op=ALU.add)
                            t_h2 = k.sig("dve", d)
                            psf[pb] = t_h2
                        wpf[slot] = t_mm
                    tn = emit_norm(H, HN, C_G2, tiles, [list(range(8))], t_h2, t_mm)
                    t_h = emit_ffn(fb, H, HN, w2i, w2o, tiles, tn[-1])
                    tnf = emit_norm(H, H, C_GF, tiles, [list(range(8))], t_h, None)
                    ca = tiles[0][0]
                    for kc in range(8):
                        k.wait("sp", tnf[-1])
                        outs.append(k.dma("sp", "out", outT[kc * 128:(kc + 1) * 128, ca:ca + 1024], H[:, kc, ca:ca + 1024]))
                for e in ("sp", "pool", "dve", "act", "pe"):
                    k.wait(e, outs[-1], global_toks)

        @block.gpsimd
        def _(g):
            _body(g)

    return nc


def _prep_inputs(inp):
    f32 = np.float32
    x = np.asarray(inp["x"], f32)
    meta = np.asarray(inp["meta_tokens"], f32)

    def lay_in(w):
        w = np.asarray(w, f32).reshape(8, 128, 2, NF, 128)
        return np.ascontiguousarray(w.transpose(3, 1, 0, 2, 4)).reshape(NF, 128, 2048)

    def lay_out(w):
        w = np.asarray(w, f32).reshape(2, 11, 128, 8, 128)
        return np.ascontiguousarray(w.transpose(0, 3, 2, 1, 4)).reshape(16, 128, 1408)

    w1i, w1o = lay_in(inp["w_ffn1_in"][0]), lay_out(inp["w_ffn1_out"][0])
    w2i, w2o = lay_in(inp["w_ffn2_in"][0]), lay_out(inp["w_ffn2_out"][0])
    w_in = np.asarray(inp["w_in"][0], f32)
    w_out = np.asarray(inp["w_out"][0], f32)
    wop = np.ascontiguousarray(w_out.reshape(8, 128, 8, 128).transpose(2, 1, 0, 3)).reshape(8, 128, 1024)

    sc = 1.0 / math.sqrt(64.0 * T)
    cc = np.arange(64)
    ang = 2.0 * np.pi * np.outer(cc, cc) / 64.0
    c64 = np.zeros((128, 2, 128), np.float64)
    for g in range(2):
        c64[g * 64:(g + 1) * 64, 0, g * 64:(g + 1) * 64] = np.cos(ang) * sc
        c64[g * 64:(g + 1) * 64, 1, g * 64:(g + 1) * 64] = -np.sin(ang) * sc
    c64 = c64.reshape(128, 256).astype(f32)
    t1 = np.arange(N1)
    a1 = 2.0 * np.pi * np.outer(t1, t1) / N1
    c1s1 = np.zeros((N1, 2, 216), np.float64)
    c1s1[:, 0, :N1], c1s1[:, 0, N1:] = np.cos(a1), -np.sin(a1)
    c1s1[:, 1, :N1], c1s1[:, 1, N1:] = np.sin(a1), np.cos(a1)
    c1s1 = c1s1.reshape(N1, 432).astype(f32)
    t2 = np.arange(N2)[:, None, None]
    k1 = np.arange(N1)[None, :, None]
    k2 = np.arange(N2)[None, None, :]
    th = 2.0 * np.pi * ((t2 * (k1 + N1 * k2)) % T) / T
    gtab = np.concatenate([np.cos(th), np.sin(th)], axis=2).reshape(N2, N1 * 152).astype(f32)
    ident = np.eye(128, dtype=f32)

    def col8(v):
        return np.asarray(v, f32).reshape(-1, 128).T

    maps = []
    for core in range(8):
        b, q = core // 4, core % 4
        ch = slice(128 * q, 128 * (q + 1))
        xT = np.ascontiguousarray(x[b, TOK * q:TOK * (q + 1), :].T)
        metaT = np.ascontiguousarray(meta.T)
        cols = np.concatenate([np.arange(128 * q, 128 * q + 128), 512 + np.arange(128 * q, 128 * q + 128),
                               1024 + np.arange(128 * q, 128 * q + 128)])
        wsl = np.ascontiguousarray(w_in[:, cols].reshape(8, 128, 384).transpose(1, 0, 2)).reshape(128, 3072)
        pv = np.zeros((128, NV), f32)
        pv[:, C_G1:C_G1 + 8] = col8(inp["norm_ffn1"][0])
        pv[:, C_GM:C_GM + 8] = col8(inp["norm_mix"][0])
        pv[:, C_G2:C_G2 + 8] = col8(inp["norm_ffn2"][0])
        pv[:, C_GF:C_GF + 8] = col8(inp["norm_final"])
        pv[:, C_CW:C_CW + 4] = np.asarray(inp["conv_w"][0], f32)[:, ch].T
        pv[:, C_CB] = np.asarray(inp["conv_b"][0], f32)[ch]
        for c, nm in ((C_BAF, "lru_ba_fwd"), (C_BXF, "lru_bx_fwd"), (C_LAMF, "lru_lambda_fwd"),
                      (C_BAB, "lru_ba_bwd"), (C_BXB, "lru_bx_bwd"), (C_LAMB, "lru_lambda_bwd"),
                      (C_FB, "fourier_b")):
            pv[:, c] = np.asarray(inp[nm][0], f32)[ch]
        pv[:, C_GL:C_GL + 4] = col8(inp["norm_lru_out"][0])
        pv[:, C_GFO:C_GFO + 4] = col8(inp["norm_fourier_out"][0])
        lruw = np.zeros((128, 4, 128), f32)
        for gi, nm in enumerate(("lru_wa_fwd", "lru_wx_fwd", "lru_wa_bwd", "lru_wx_bwd")):
            w = np.asarray(inp[nm][0], f32)
            for hh in range(2):
                lruw[hh * 64:(hh + 1) * 64, gi, hh * 64:(hh + 1) * 64] = w[2 * q + hh]
        fwbd = np.zeros((128, 128), f32)
        fw = np.asarray(inp["fourier_w"][0], f32)
        for hh in range(2):
            fwbd[hh * 64:(hh + 1) * 64, hh * 64:(hh + 1) * 64] = fw[2 * q + hh]
        ti_ = np.arange(4)[None, :, None]
        kc_ = np.arange(8)[None, None, :]
        p_ = np.arange(128)[:, None, None]
        u_ = ((ti_ // 2) * 2 + (kc_ // 4)) * 2 + (q // 2)
        idx = (u_ * 2048 + (kc_ % 4) * 512 + (q % 2) * 256 + (ti_ % 2) * 128 + p_).reshape(128, 32).astype(np.uint32)
        maps.append({"xT": xT, "metaT": metaT, "w1i": w1i, "w1o": w1o, "w2i": w2i, "w2o": w2o, "wsl": wsl,
                     "wop": wop, "pvec": pv, "lruw": lruw.reshape(128, 512), "fwbd": fwbd, "c64": c64,
                     "ident": ident, "c1s1": c1s1, "gtab": gtab, "idx": idx})
    return maps


_NC_CACHE = {}


def run(inp, debug=False, stop=None):
    if (debug, stop) not in _NC_CACHE:
        _NC_CACHE[(debug, stop)] = build_nc(debug, stop)
    nc = _NC_CACHE[(debug, stop)]
    maps = _prep_inputs(inp)
    res = run_bass_kernel_spmd(nc, maps, core_ids=list(range(8)))
    return res


def kernel(**inputs):
    res = run(inputs, DEBUG)
    out = np.empty((2, SEQ, D), np.float32)
    for core in range(8):
        b, q = core // 4, core % 4
        out[b, TOK * q:TOK * (q + 1), :] = np.asarray(res.results[core]["outT"]).T
    return out
```
